# Optimizing a Trainium2 kernel written in Bass

```python
import jax, jax.numpy as jnp
from jax import lax
import numpy as np

D_MODEL = 1024
BATCH = 8
SEQ = 2048
DEPTH = 1

HEAD_DIM = 64
HEADS_PER_GROUP = 8
ATTN_PATTERNS = ((128, 1), (512, 4), (2048, 16))
N_ATTN_GROUPS = len(ATTN_PATTERNS)
ATTN_WIDTH = N_ATTN_GROUPS * HEADS_PER_GROUP * HEAD_DIM
ATTN_OUT = HEADS_PER_GROUP * HEAD_DIM
ATTN_BLOCK = 128
ROPE_THETA = 10000.0
GMLP_CHUNK = 128
GMLP_GROUPS = 8
GMLP_WIDTH = D_MODEL
GMLP_GROUP_DIM = GMLP_WIDTH // GMLP_GROUPS
N_BRANCHES = 2
IN_WIDTH = 3 * ATTN_WIDTH + 2 * GMLP_WIDTH + N_BRANCHES * D_MODEL
D_FF = 2816
ALPHA = (2 * DEPTH) ** 0.25
BETA = (8 * DEPTH) ** -0.25
LN_EPS = 1e-5

kernel_name = 'hybrid_dilated_attn_gmlp_macaron_deepnorm'


def layer_norm(x, g, b):
    xf = x.astype(jnp.float32)
    mu = jnp.mean(xf, -1, keepdims=True)
    var = jnp.mean(jnp.square(xf - mu), -1, keepdims=True)
    y = (xf - mu) * lax.rsqrt(var + LN_EPS) * g.astype(jnp.float32) + b.astype(jnp.float32)
    return y.astype(x.dtype)


def swiglu_ffn(x, w_gate, w_up, w_down):
    return (jax.nn.silu(x @ w_gate) * (x @ w_up)) @ w_down


def rotary(t, cos, sin):
    tf = t.astype(jnp.float32)
    t1, t2 = jnp.split(tf, 2, axis=-1)
    c = cos[:, :, None, None, :]
    s = sin[:, :, None, None, :]
    return jnp.concatenate([t1 * c - t2 * s, t2 * c + t1 * s], axis=-1).astype(t.dtype)


def dilated_window_attention(q, k, v, window, dilation):
    b, s, h, dh = q.shape
    w = window // dilation
    sub_len = s // dilation
    n_blk = -(-sub_len // ATTN_BLOCK)
    pad = n_blk * ATTN_BLOCK - sub_len

    def to_blocks(t):
        t = t.reshape(b, sub_len, dilation, h, dh).transpose(0, 2, 1, 3, 4)
        t = jnp.pad(t, ((0, 0), (0, 0), (0, pad), (0, 0), (0, 0)))
        return t.reshape(b, dilation, n_blk, ATTN_BLOCK, h, dh)

    def with_prev(t):
        prev = jnp.pad(t, ((0, 0), (0, 0), (1, 0), (0, 0), (0, 0), (0, 0)))[:, :, :-1]
        return jnp.concatenate([prev, t], axis=3)

    qb = to_blocks(q)
    kc = with_prev(to_blocks(k))
    vc = with_prev(to_blocks(v))
    scores = jnp.einsum('brnqhd,brnkhd->brnhqk', qb, kc,
                        preferred_element_type=jnp.float32) * (dh ** -0.5)
    blk = np.arange(n_blk)[:, None, None]
    qi = np.arange(ATTN_BLOCK)[None, :, None]
    kj = np.arange(2 * ATTN_BLOCK)[None, None, :]
    dist = qi + ATTN_BLOCK - kj
    kpos = (blk - 1) * ATTN_BLOCK + kj
    mask = (dist >= 0) & (dist <= w) & (kpos >= 0)
    scores = jnp.where(jnp.asarray(mask)[None, None, :, None], scores, -jnp.inf)
    m = jnp.max(scores, axis=-1, keepdims=True)
    p = jnp.exp(scores - m)
    l = jnp.sum(p, axis=-1, keepdims=True)
    o = jnp.einsum('brnhqk,brnkhd->brnqhd', p / l, vc.astype(jnp.float32))
    lse = (m + jnp.log(l))[..., 0].transpose(0, 1, 2, 4, 3)

    def from_blocks(t):
        t = t.reshape((b, dilation, n_blk * ATTN_BLOCK) + t.shape[4:])[:, :, :sub_len]
        t = jnp.swapaxes(t, 1, 2)
        return t.reshape((b, s) + t.shape[3:])

    return from_blocks(o), from_blocks(lse)


def hybrid_mixer(h, cos, sin, w_in, b_gates, gmlp_ln_g, gmlp_ln_b, gmlp_w_s, gmlp_b_s,
                 w_attn_branch, w_gmlp_branch, w_out):
    b, s, _ = h.shape
    proj = h @ w_in
    qkv, z, g = jnp.split(proj, [3 * ATTN_WIDTH, 3 * ATTN_WIDTH + 2 * GMLP_WIDTH], axis=-1)

    qkv = qkv.reshape(b, s, 3, N_ATTN_GROUPS, HEADS_PER_GROUP, HEAD_DIM)
    q = rotary(qkv[:, :, 0], cos, sin)
    k = rotary(qkv[:, :, 1], cos, sin)
    v = qkv[:, :, 2]
    outs, lses = [], []
    for gi, (window, dilation) in enumerate(ATTN_PATTERNS):
        o, lse = dilated_window_attention(q[:, :, gi], k[:, :, gi], v[:, :, gi], window, dilation)
        outs.append(o)
        lses.append(lse)
    wts = jax.nn.softmax(jnp.stack(lses), axis=0)
    y_attn = jnp.sum(wts[..., None] * jnp.stack(outs), axis=0).reshape(b, s, ATTN_OUT).astype(h.dtype)

    u, vg = jnp.split(jax.nn.gelu(z, approximate=False), 2, axis=-1)
    vg = layer_norm(vg, gmlp_ln_g, gmlp_ln_b)
    n_chunk = s // GMLP_CHUNK
    vg = vg.reshape(b, n_chunk, GMLP_CHUNK, GMLP_GROUPS, GMLP_GROUP_DIM)
    w_s = gmlp_w_s * jnp.tril(jnp.ones((GMLP_CHUNK, GMLP_CHUNK), gmlp_w_s.dtype))
    mixed = jnp.einsum('gts,bnsgc->bntgc', w_s, vg) + gmlp_b_s.T[:, :, None]
    y_gmlp = u * mixed.reshape(b, s, GMLP_WIDTH)

    branches = jnp.stack([y_attn @ w_attn_branch, y_gmlp @ w_gmlp_branch], axis=2)
    gates = jax.nn.sigmoid(g.reshape(b, s, N_BRANCHES, D_MODEL) + b_gates.reshape(N_BRANCHES, D_MODEL))
    return jnp.sum(gates * branches, axis=2) @ w_out


def setup_inputs(seed: int = 0) -> dict:
    key = jax.random.key(seed)
    ks = jax.random.split(key, 32)

    def nrm(k, shape, scale):
        return jax.random.normal(k, shape, jnp.float32) * scale

    d_s = D_MODEL ** -0.5
    w_in = jnp.concatenate([
        nrm(ks[2], (DEPTH, D_MODEL, ATTN_WIDTH), d_s),
        nrm(ks[3], (DEPTH, D_MODEL, ATTN_WIDTH), d_s),
        nrm(ks[4], (DEPTH, D_MODEL, ATTN_WIDTH), BETA * d_s),
        nrm(ks[5], (DEPTH, D_MODEL, 2 * GMLP_WIDTH), d_s),
        nrm(ks[6], (DEPTH, D_MODEL, N_BRANCHES * D_MODEL), d_s),
    ], axis=-1)
    return {
        'x': jax.random.normal(ks[0], (BATCH, SEQ, D_MODEL), jnp.float32),
        'positions': jnp.broadcast_to(jnp.arange(SEQ, dtype=jnp.int32), (BATCH, SEQ)),
        'ffn1_w_gate': nrm(ks[7], (DEPTH, D_MODEL, D_FF), d_s),
        'ffn1_w_up': nrm(ks[8], (DEPTH, D_MODEL, D_FF), d_s),
        'ffn1_w_down': nrm(ks[9], (DEPTH, D_FF, D_MODEL), BETA * D_FF ** -0.5),
        'ln1_g': 1.0 + nrm(ks[10], (DEPTH, D_MODEL), 0.02),
        'ln1_b': nrm(ks[11], (DEPTH, D_MODEL), 0.02),
        'w_in': w_in,
        'b_gates': nrm(ks[12], (DEPTH, N_BRANCHES * D_MODEL), 0.02),
        'gmlp_ln_g': 1.0 + nrm(ks[13], (DEPTH, GMLP_WIDTH), 0.02),
        'gmlp_ln_b': nrm(ks[14], (DEPTH, GMLP_WIDTH), 0.02),
        'gmlp_w_s': nrm(ks[15], (DEPTH, GMLP_GROUPS, GMLP_CHUNK, GMLP_CHUNK), 0.5 * GMLP_CHUNK ** -0.5),
        'gmlp_b_s': 1.0 + nrm(ks[16], (DEPTH, GMLP_GROUPS, GMLP_CHUNK), 0.02),
        'w_attn_branch': nrm(ks[17], (DEPTH, ATTN_OUT, D_MODEL), BETA * ATTN_OUT ** -0.5),
        'w_gmlp_branch': nrm(ks[18], (DEPTH, GMLP_WIDTH, D_MODEL), BETA * GMLP_WIDTH ** -0.5),
        'w_out': nrm(ks[19], (DEPTH, D_MODEL, D_MODEL), BETA * d_s),
        'ln2_g': 1.0 + nrm(ks[20], (DEPTH, D_MODEL), 0.02),
        'ln2_b': nrm(ks[21], (DEPTH, D_MODEL), 0.02),
        'ffn2_w_gate': nrm(ks[22], (DEPTH, D_MODEL, D_FF), d_s),
        'ffn2_w_up': nrm(ks[23], (DEPTH, D_MODEL, D_FF), d_s),
        'ffn2_w_down': nrm(ks[24], (DEPTH, D_FF, D_MODEL), BETA * D_FF ** -0.5),
        'ln3_g': 1.0 + nrm(ks[25], (DEPTH, D_MODEL), 0.02),
        'ln3_b': nrm(ks[26], (DEPTH, D_MODEL), 0.02),
    }


def reference(x, positions, ffn1_w_gate, ffn1_w_up, ffn1_w_down, ln1_g, ln1_b, w_in, b_gates,
              gmlp_ln_g, gmlp_ln_b, gmlp_w_s, gmlp_b_s, w_attn_branch, w_gmlp_branch, w_out,
              ln2_g, ln2_b, ffn2_w_gate, ffn2_w_up, ffn2_w_down, ln3_g, ln3_b):
    inv_freq = ROPE_THETA ** (-jnp.arange(0, HEAD_DIM, 2, dtype=jnp.float32) / HEAD_DIM)
    ang = positions.astype(jnp.float32)[..., None] * inv_freq
    cos, sin = jnp.cos(ang), jnp.sin(ang)
    h = x
    for l in range(DEPTH):
        h = layer_norm(ALPHA * h + 0.5 * swiglu_ffn(h, ffn1_w_gate[l], ffn1_w_up[l], ffn1_w_down[l]),
                       ln1_g[l], ln1_b[l])
        mix = hybrid_mixer(h, cos, sin, w_in[l], b_gates[l], gmlp_ln_g[l], gmlp_ln_b[l], gmlp_w_s[l],
                           gmlp_b_s[l], w_attn_branch[l], w_gmlp_branch[l], w_out[l])
        h = layer_norm(ALPHA * h + mix, ln2_g[l], ln2_b[l])
        h = layer_norm(ALPHA * h + 0.5 * swiglu_ffn(h, ffn2_w_gate[l], ffn2_w_up[l], ffn2_w_down[l]),
                       ln3_g[l], ln3_b[l])
    return h
```

```python
import contextlib
import numpy as np
import concourse.bass as bass
import concourse.mybir as mybir
from concourse.bass_utils import run_bass_kernel_spmd

F32 = mybir.dt.float32
BF16 = mybir.dt.bfloat16
I32 = mybir.dt.int32
AF = mybir.ActivationFunctionType
ALU = mybir.AluOpType

D = 1024
T = 2048
NB = 16
DFF = 2816
ALPHA = 2.0 ** 0.25
LN_EPS = 1e-5
EPS2 = LN_EPS / (ALPHA * ALPHA)
QUARTERS = [(0, 6), (6, 6), (12, 5), (17, 5)]
DIL = (1, 4, 16)
POOL_DMA_INFLIGHT = 3
CFG = {}


class _Stop(Exception):
    pass


def ck(name):
    if CFG.get('stop') == name:
        raise _Stop()


class H:
    __slots__ = ("name", "lw", "rde", "rdd", "dsem")

    def __init__(self, name, inherit=None):
        self.name = name
        self.lw = None
        self.rde = {}
        self.rdd = list(inherit) if inherit else []
        self.dsem = None

    def pending(self):
        r = list(self.rde.values()) + list(self.rdd)
        if self.lw is not None:
            r.append(self.lw)
        return r


class DSem:
    __slots__ = ("sem", "count")

    def __init__(self, sem):
        self.sem = sem
        self.count = 0


class Op:
    __slots__ = ("eng", "fn", "deps", "signal", "ticket", "is_dma", "dsem", "waits")

    def __init__(self, eng, fn, is_dma=False, dsem=None):
        self.eng = eng
        self.fn = fn
        self.deps = []
        self.signal = False
        self.ticket = None
        self.is_dma = is_dma
        self.dsem = dsem
        self.waits = None


ENGS = ("pe", "act", "dve", "pool", "sp")


class Sched:
    def __init__(self, nc, stack):
        self.nc = nc
        self.stack = stack
        self.ops = {e: [] for e in ENGS}
        self.esem = {e: stack.enter_context(nc.semaphore("s_" + e)) for e in ENGS}
        self.n_dsem = 0
        self.pool_dmas = []

    def new_dsem(self):
        s = self.stack.enter_context(self.nc.semaphore("d%d" % self.n_dsem))
        self.n_dsem += 1
        return DSem(s)

    def _record(self, op, reads, writes):
        deps = []
        for r in reads:
            if r.lw is not None:
                deps.append((r.lw, "raw"))
            for x in r.rdd:
                if x.is_dma and False:
                    pass
        for w in writes:
            if w.lw is not None:
                deps.append((w.lw, "waw"))
            for x in w.rde.values():
                deps.append((x, "war"))
            for x in w.rdd:
                deps.append((x, "war"))
        seen = set()
        for d, kind in deps:
            if d is op or id(d) in seen:
                continue
            if (not d.is_dma) and (not op.is_dma) and d.eng == op.eng:
                if op.eng == "pe" or (kind == "war" and not CFG.get("selfwar", 1)):
                    continue
            seen.add(id(d))
            op.deps.append(d)
            d.signal = True
        for r in reads:
            if op.is_dma:
                r.rdd.append(op)
            else:
                r.rde[op.eng] = op
        for w in writes:
            w.lw = op
            w.rde = {}
            w.rdd = []
        self.ops[op.eng].append(op)
        return op

    def op(self, eng, fn, reads=(), writes=()):
        return self._record(Op(eng, fn), list(reads), list(writes))

    def dma(self, eng, fn, reads=(), writes=(), dsem=None):
        if dsem is None:
            w0 = writes[0]
            if w0.dsem is None:
                w0.dsem = self.new_dsem()
            dsem = w0.dsem
        op = Op(eng, fn, is_dma=True, dsem=dsem)
        op.signal = True
        if eng == "pool":
            if CFG.get('cap', 1) and len(self.pool_dmas) >= POOL_DMA_INFLIGHT:
                op.deps.append(self.pool_dmas[-POOL_DMA_INFLIGHT])
            self.pool_dmas.append(op)
        return self._record(op, list(reads), list(writes))

    def finalize(self):
        nc = self.nc
        for e in ENGS:
            cnt = 0
            for op in self.ops[e]:
                if op.is_dma:
                    op.dsem.count += 16
                    op.ticket = (op.dsem.sem, op.dsem.count)
                elif op.signal:
                    cnt += 1
                    op.ticket = (self.esem[e], cnt)
        for e in ENGS:
            waited = {}
            for op in self.ops[e]:
                need = {}
                for d in op.deps:
                    sem, val = d.ticket
                    k = id(sem)
                    if k not in need or need[k][1] < val:
                        need[k] = (sem, val)
                w = []
                for k, (sem, val) in need.items():
                    if waited.get(k, 0) >= val:
                        continue
                    waited[k] = val
                    w.append((sem, val))
                op.waits = w

        def emit(ename, eng):
            for op in self.ops[ename]:
                for sem, val in op.waits:
                    eng.wait_ge(sem, val)
                ins = op.fn(eng)
                if op.is_dma:
                    ins.then_inc(op.dsem.sem, 16)
                elif op.signal:
                    ins.then_inc(self.esem[ename], 1)

        with nc.Block() as block:
            @block.tensor
            def _(eng):
                emit("pe", eng)

            @block.scalar
            def _(eng):
                emit("act", eng)

            @block.vector
            def _(eng):
                emit("dve", eng)

            @block.gpsimd
            def _(eng):
                emit("pool", eng)

            @block.sync
            def _(eng):
                emit("sp", eng)


class Arena:
    def __init__(self, ap_f32):
        self.ap = ap_f32
        self.live = []

    def tile(self, name, off, nbytes, dtype, pattern=None, **kw):
        assert off % 4 == 0 and nbytes % 4 == 0
        inherit = []
        for t in self.live:
            if t.off < off + nbytes and off < t.off + t.nbytes:
                for h in t.hs.values():
                    inherit.extend(h.pending())
        v = self.ap[:, off // 4:(off + nbytes) // 4]
        if dtype != F32:
            v = v.bitcast(dtype)
        if pattern is not None:
            v = v.rearrange(pattern, **kw)
        t = ATile(name, off, nbytes, v, inherit)
        self.live.append(t)
        return t


class ATile:
    def __init__(self, name, off, nbytes, ap, inherit):
        self.name = name
        self.off = off
        self.nbytes = nbytes
        self.ap = ap
        self.inherit = inherit
        self.hs = {}

    def h(self, key=0):
        if key not in self.hs:
            self.hs[key] = H("%s.%s" % (self.name, key), self.inherit)
        return self.hs[key]

    def hall(self, keys):
        return [self.h(k) for k in keys]


def build_program():
    nc = bass.Bass("TRN2", target_bir_lowering=False)

    def din(name, shape, dt=F32):
        return nc.dram_tensor(name, shape, dt, kind="ExternalInput").ap()

    xT = din("xT", [D, T])
    x = din("x", [T, D])
    pos3 = din("pos3", [128, 48], I32)
    w1g = din("w1g", [D, DFF]); w1u = din("w1u", [D, DFF]); w1d = din("w1d", [DFF, D])
    w2g = din("w2g", [D, DFF]); w2u = din("w2u", [D, DFF]); w2d = din("w2d", [DFF, D])
    ln1g = din("ln1g", [D]); ln1b = din("ln1b", [D])
    ln2g = din("ln2g", [D]); ln2b = din("ln2b", [D])
    ln3g = din("ln3g", [D]); ln3b = din("ln3b", [D])
    w_in = din("w_in", [D, 8704])
    bgT = din("bgT", [128, 16])
    glngT = din("glngT", [128, 8]); glnbT = din("glnbT", [128, 8])
    wsT = din("wsT", [128, 8, 128])
    gbs = din("gbs", [8 * 128])
    wab = din("wab", [512, D]); wgb = din("wgb", [D, D]); wout = din("wout", [D, D])
    c_ident = din("c_ident", [128, 128]); c_mask = din("c_mask", [128, 256]); c_tril = din("c_tril", [128, 128])
    c_invf = din("c_invf", [128, 32]); c_sel = din("c_sel", [128, 64]); c_hm = din("c_hm", [128, 2])
    out = nc.dram_tensor("out", [T, D], F32, kind="ExternalOutput").ap()
    spill = nc.dram_tensor("spill", [128, NB * D], F32, kind="Internal").ap()
    hspill = H("spill")

    with contextlib.ExitStack() as st:
        S = Sched(nc, st)
        ARENA_BYTES = 65536 + 32768 + 110592
        arena_t = st.enter_context(nc.sbuf_tensor("arena", [128, ARENA_BYTES // 4], F32))
        AR = Arena(arena_t[:])
        SCR = 65536 + 32768

        def sbt(name, shape, dt):
            return st.enter_context(nc.sbuf_tensor(name, shape, dt))

        ident_b = sbt("ident_b", [128, 128], BF16); h_ident = H("ident")
        mask_b = sbt("mask_b", [128, 256], BF16); h_mask = H("mask")
        ones_b = sbt("ones_b", [128, 128], BF16); h_ones = H("ones")
        sel_f = sbt("sel_f", [128, 64], F32); h_sel = H("sel")
        eps_t = sbt("eps_t", [128, 1], F32); h_eps = H("eps")
        pi_t = sbt("pi_t", [128, 1], F32); h_pi = H("pi")
        hm_t = sbt("hm_t", [128, 2], F32); h_hm = H("hm")
        bg_t = sbt("bg_t", [128, 16], F32); h_bg = H("bg")
        glng_t = sbt("glng_t", [128, 8], F32); glnb_t = sbt("glnb_t", [128, 8], F32); h_gln = H("gln")

        PP = [st.enter_context(nc.psum_tensor("pp%d" % i, [128, 1024], F32)) for i in range(4)]
        hB = [H("bank%d" % i) for i in range(8)]

        def bank(i):
            return PP[i // 2][:, (i % 2) * 512:(i % 2) * 512 + 512]

        def mm(o, lhsT, rhs, start, stop, reads, writes):
            S.op("pe", lambda e: e.matmul(o, lhsT=lhsT, rhs=rhs, start=start, stop=stop), reads, writes)

        def tr(o, in_, reads, writes):
            S.op("pe", lambda e: e.transpose(o, in_, ident_b[:]), list(reads) + [h_ident], writes)

        def act(o, in_, func, reads, writes, bias=None, scale=None):
            kw = {}
            if bias is not None:
                kw["bias"] = bias
            if scale is not None:
                kw["scale"] = scale
            S.op("act", lambda e: e.activation(out=o, in_=in_, func=func, **kw), reads, writes)

        def tt(eng, o, a, b, op, reads, writes):
            S.op(eng, lambda e: e.tensor_tensor(out=o, in0=a, in1=b, op=op), reads, writes)

        def ts(eng, o, a, s1, s2, op0, op1, reads, writes):
            if op1 is None:
                S.op(eng, lambda e: e.tensor_scalar(out=o, in0=a, scalar1=s1, scalar2=None, op0=op0), reads, writes)
            else:
                S.op(eng, lambda e: e.tensor_scalar(out=o, in0=a, scalar1=s1, scalar2=s2, op0=op0, op1=op1), reads, writes)

        def stt(o, a, sc, b, op0, op1, reads, writes):
            S.op("dve", lambda e: e.scalar_tensor_tensor(out=o, in0=a, scalar=sc, in1=b, op0=op0, op1=op1), reads, writes)

        def cp(eng, o, a, reads, writes):
            S.op(eng, lambda e: e.tensor_copy(out=o, in_=a), reads, writes)

        def memset(eng, o, val, writes):
            S.op(eng, lambda e: e.memset(o, val), [], writes)

        def dma(eng, o, a, reads, writes, dsem=None):
            S.dma(eng, lambda e: e.dma_start(out=o, in_=a), reads, writes, dsem=dsem)

        try:
            dma("pool", ident_b[:], c_ident, [], [h_ident])
            dma("pool", mask_b[:], c_mask, [], [h_mask])
            dma("sp", sel_f[:], c_sel, [], [h_sel])
            dma("sp", hm_t[:], c_hm, [], [h_hm])
            dma("sp", bg_t[:], bgT, [], [h_bg])
            dma("sp", glng_t[:], glngT, [], [h_gln])
            dma("sp", glnb_t[:], glnbT, [], [h_gln])
            memset("dve", ones_b[:], 1.0, [h_ones])
            memset("dve", eps_t[:], EPS2, [h_eps])
            memset("dve", pi_t[:], float(np.pi), [h_pi])

            Rt = AR.tile("R", 0, 65536, F32, "p (b f) -> p b f", b=NB)
            At = AR.tile("A", 65536, 32768, BF16, "p (k t) -> p k t", k=8)
            R = Rt.ap
            A = At.ap

            def hA_tg(tg):
                return At.hall(range(4 * tg, 4 * tg + 4))

            def hA_all():
                return At.hall(range(16))

            xTv = xT.rearrange("(k p) t -> p k t", p=128)
            for j in range(8):
                dma("pool", A[:, :, j * 256:(j + 1) * 256], xTv[:, :, j * 256:(j + 1) * 256], [], At.hall([2 * j, 2 * j + 1]))
            xv = x.rearrange("(b p) f -> p b f", p=128)
            for b4 in range(4):
                dma("sp", R[:, b4 * 4:(b4 + 1) * 4, :], xv[:, b4 * 4:(b4 + 1) * 4, :], [], Rt.hall(range(b4 * 4, b4 * 4 + 4)))

            def ffn(wg, wu, wd, tagp, on_last_block=None):
                actT = AR.tile(tagp + "actT", SCR + 0, 24576, BF16, "p (c t) -> p c t", c=6)
                Wg = AR.tile(tagp + "Wg", SCR + 24576, 12288, BF16, "p (k n) -> p k n", k=8)
                Wu = AR.tile(tagp + "Wu", SCR + 36864, 12288, BF16, "p (k n) -> p k n", k=8)
                Wd = AR.tile(tagp + "Wd", SCR + 49152, 12288, BF16, "p (c n) -> p c n", c=6)
                sil = AR.tile(tagp + "sil", SCR + 61440, 4096, F32, "p (s n) -> p s n", s=2)
                wgv = wg.rearrange("(k p) n -> p k n", p=128)
                wuv = wu.rearrange("(k p) n -> p k n", p=128)
                wdv = wd.rearrange("(c p) n -> p c n", p=128)
                cnt = 0
                for (c0, ncq) in QUARTERS:
                    ncol = ncq * 128
                    dma("pool", Wg.ap[:, :, 0:ncol], wgv[:, :, c0 * 128:c0 * 128 + ncol], [], [Wg.h()])
                    dma("pool", Wu.ap[:, :, 0:ncol], wuv[:, :, c0 * 128:c0 * 128 + ncol], [], [Wu.h()])
                    dma("pool", Wd.ap[:, 0:ncq, :], wdv[:, c0:c0 + ncq, :], [], [Wd.h()])
                    for c in range(ncq):
                        for tg in range(4):
                            bg_, bu_ = cnt % 2, 2 + cnt % 2
                            sl = cnt % 2
                            cnt += 1
                            for kc in range(8):
                                mm(bank(bg_), Wg.ap[:, kc, c * 128:(c + 1) * 128], A[:, kc, tg * 512:(tg + 1) * 512],
                                   kc == 0, kc == 7, [Wg.h()] + hA_tg(tg), [hB[bg_]])
                            for kc in range(8):
                                mm(bank(bu_), Wu.ap[:, kc, c * 128:(c + 1) * 128], A[:, kc, tg * 512:(tg + 1) * 512],
                                   kc == 0, kc == 7, [Wu.h()] + hA_tg(tg), [hB[bu_]])
                            act(sil.ap[:, sl, :], bank(bg_), AF.Silu, [hB[bg_]], [sil.h(sl)])
                            tt("dve", actT.ap[:, c, tg * 512:(tg + 1) * 512], sil.ap[:, sl, :], bank(bu_), ALU.mult,
                               [sil.h(sl), hB[bu_]], [actT.h((c, tg))])
                    for blk in range(NB):
                        pb = 4 + 2 * (blk % 2)
                        for hf in range(2):
                            for c in range(ncq):
                                mm(bank(pb + hf), actT.ap[:, c, blk * 128:(blk + 1) * 128], Wd.ap[:, c, hf * 512:(hf + 1) * 512],
                                   c == 0, c == ncq - 1, [actT.h((c, blk // 4)), Wd.h()], [hB[pb + hf]])
                        for hf in range(2):
                            stt(R[:, blk, hf * 512:(hf + 1) * 512], bank(pb + hf), 0.5 / ALPHA, R[:, blk, hf * 512:(hf + 1) * 512],
                                ALU.mult, ALU.add, [hB[pb + hf], Rt.h(blk)], [Rt.h(blk)])
                        if on_last_block is not None and c0 == QUARTERS[-1][0]:
                            on_last_block(blk)

            LNO = SCR + 97280
            hout_dsem = S.new_dsem()
            hout_blk = [H("out%d" % i) for i in range(NB)]

            def layernorm(g_ap, b_ap, tagp, to_A=True, to_out=False):
                Gb = AR.tile(tagp + "Gb", LNO, 4096, F32)
                Bb = AR.tile(tagp + "Bb", LNO + 4096, 4096, F32)
                hb = AR.tile(tagp + "hb", LNO + 8192, 4096, BF16, "p (s n) -> p s n", s=2)
                sm = AR.tile(tagp + "sm", LNO + 12288, 1024, F32, "p (b n) -> p b n", b=NB)
                dma("sp", Gb.ap, g_ap.partition_broadcast(128), [], [Gb.h()])
                dma("sp", Bb.ap, b_ap.partition_broadcast(128), [], [Bb.h()])

                def stage1(blk):
                    hs = sm.h(blk)
                    S.op("dve", lambda e: e.bn_stats(out=sm.ap[:, blk, 0:6], in_=R[:, blk, 0:512]), [Rt.h(blk)], [hs])
                    S.op("dve", lambda e: e.bn_stats(out=sm.ap[:, blk, 6:12], in_=R[:, blk, 512:1024]), [Rt.h(blk)], [hs])
                    S.op("dve", lambda e: e.bn_aggr(out=sm.ap[:, blk, 12:14], in_=sm.ap[:, blk, 0:12]), [hs], [hs])
                    act(sm.ap[:, blk, 14:15], sm.ap[:, blk, 13:14], AF.Sqrt, [hs, h_eps], [hs], bias=eps_t[:], scale=1.0)

                def stage2(blk):
                    hs = sm.h(blk)
                    S.op("dve", lambda e: e.reciprocal(out=sm.ap[:, blk, 15:16], in_=sm.ap[:, blk, 14:15]), [hs], [hs])
                    stt(R[:, blk, :], R[:, blk, :], sm.ap[:, blk, 12:13], Gb.ap, ALU.subtract, ALU.mult, [Rt.h(blk), hs, Gb.h()], [Rt.h(blk)])
                    stt(R[:, blk, :], R[:, blk, :], sm.ap[:, blk, 15:16], Bb.ap, ALU.mult, ALU.add, [Rt.h(blk), hs, Bb.h()], [Rt.h(blk)])
                    if to_out:
                        dma("sp", out[blk * 128:(blk + 1) * 128, :], R[:, blk, :], [Rt.h(blk)], [hout_blk[blk]], dsem=hout_dsem)
                    if to_A:
                        s = blk % 2
                        act(hb.ap[:, s, :], R[:, blk, :], AF.Copy, [Rt.h(blk)], [hb.h(s)])
                        tb = blk % 2
                        tpv = bank(tb).bitcast(BF16).rearrange("p (k n) -> p k n", k=8)
                        for kc in range(8):
                            tr(tpv[:, kc, :], hb.ap[:, s, kc * 128:(kc + 1) * 128], [hb.h(s)], [hB[tb]])

                def stage3(blk):
                    tb = blk % 2
                    tpv = bank(tb).bitcast(BF16).rearrange("p (k n) -> p k n", k=8)
                    cp("dve", A[:, :, blk * 128:(blk + 1) * 128], tpv, [hB[tb]], [At.h(blk)])

                if CFG.get("ln_blockwise") and not to_A:
                    return stage1, stage2
                for i in range(NB + 2):
                    if i < NB:
                        stage1(i)
                    if 0 <= i - 1 < NB:
                        stage2(i - 1)
                    if to_A and 0 <= i - 2 < NB:
                        stage3(i - 2)

            ffn(w1g, w1u, w1d, "f1")
            ck("ffn1")
            layernorm(ln1g, ln1b, "l1")
            ck("ln1")
            for b4 in range(4):
                dma("sp", spill[:, b4 * 4 * D:(b4 + 1) * 4 * D], arena_t[:, b4 * 4 * D:(b4 + 1) * 4 * D],
                    Rt.hall(range(b4 * 4, b4 * 4 + 4)), [hspill])

            yat = AR.tile("yat", 0, 32768, BF16, "p (h t) -> p h t", h=8)
            ygm = AR.tile("ygm", 32768, 32768, BF16, "p (k t) -> p k t", k=8)
            Wqkv = AR.tile("Wqkv", SCR + 0, 12288, BF16, "p (u j k n) -> p u j k n", u=2, j=3, k=8)
            qkT = AR.tile("qkT", SCR + 12288, 24576, BF16, "p (u j t) -> p u j t", u=2, j=3)
            Vaug = AR.tile("Vaug", SCR + 36864, 8448, BF16, "p (u b h c) -> p u b h c", u=2, b=NB, h=2)
            PTb = AR.tile("PTb", SCR + 61696, 2048, BF16, "p (s n) -> p s n", s=4)
            rot1 = AR.tile("rot1", SCR + 63744, 2048, F32, "p (s j n) -> p s j n", s=2, j=2)
            rot2 = AR.tile("rot2", SCR + 65792, 2048, F32, "p (s j n) -> p s j n", s=2, j=2)
            rotb = AR.tile("rotb", SCR + 67840, 1024, BF16, "p (s j n) -> p s j n", s=2, j=2)
            Ct = AR.tile("Ct", SCR + 68864, 6144, F32, "p (b f) -> p b f", b=48)
            St = AR.tile("St", SCR + 75008, 6144, F32, "p (b f) -> p b f", b=48)
            Sn = AR.tile("Sn", SCR + 81152, 6144, F32, "p (b f) -> p b f", b=48)
            rL = AR.tile("rL", SCR + 87296, 2048, F32)
            lnL = AR.tile("lnL", SCR + 89344, 2048, F32)
            posi = AR.tile("posi", SCR + 91392, 192, I32)
            posf = AR.tile("posf", SCR + 91584, 192, F32)
            invf = AR.tile("invf", SCR + 91776, 128, F32)
            angi = AR.tile("angi", SCR + 45312, 6144, I32, "p (b f) -> p b f", b=48)

            dma("sp", posi.ap, pos3, [], [posi.h()])
            dma("sp", invf.ap, c_invf, [], [invf.h()])
            cp("dve", posf.ap, posi.ap, [posi.h()], [posf.h()])
            for tab, off in ((St, 0.0), (Ct, 0.25)):
                tt("dve", tab.ap, posf.ap.unsqueeze(2).to_broadcast([128, 48, 32]), invf.ap.unsqueeze(1).to_broadcast([128, 48, 32]),
                   ALU.mult, [posf.h(), invf.h()], [tab.h()])
                if off != 0.0:
                    ts("dve", tab.ap, tab.ap, off, None, ALU.add, None, [tab.h()], [tab.h()])
                cp("dve", angi.ap, tab.ap, [tab.h()], [angi.h()])
                cp("dve", Sn.ap, angi.ap, [angi.h()], [Sn.h()])
                tt("dve", tab.ap, tab.ap, Sn.ap, ALU.subtract, [tab.h(), Sn.h()], [tab.h()])
                stt(tab.ap, tab.ap, 0.0, tab.ap, ALU.is_lt, ALU.add, [tab.h()], [tab.h()])
                act(tab.ap, tab.ap, AF.Sin, [tab.h(), h_pi], [tab.h()], bias=pi_t[:], scale=float(-2.0 * np.pi))
            ts("dve", Sn.ap, St.ap, -1.0, None, ALU.mult, None, [St.h()], [Sn.h()])
            ck("tables")

            acc = AR.tile("acc", SCR + 45312, 16384, F32, "p (h t) -> p h t", h=2)
            memset("dve", Vaug.ap.rearrange("p u b h c -> p (u b h) c")[:, :, 64:66], 1.0, [Vaug.h(0), Vaug.h(1)])
            win_v = w_in.rearrange("(k p) n -> p k n", p=128)

            def tok_sel(g, n, kc):
                if g == 0:
                    return A[:, kc, n * 128:(n + 1) * 128], [At.h(n)]
                if g == 1:
                    r, n2 = n // 4, n % 4
                    s0 = n2 * 512 + r
                    return A[:, kc, s0:s0 + 509:4], At.hall(range(4 * n2, 4 * n2 + 4))
                return A[:, kc, n:T:16], hA_all()

            def has_next(g, n):
                return (g == 0 and n < 15) or (g == 1 and n % 4 < 3)

            def has_prev(g, n):
                return (g == 0 and n > 0) or (g == 1 and n % 4 > 0)

            units = [(hp, g) for hp in range(4) for g in range(3)]
            slot_hj = {(i, j): H("slot%d_%d" % (i, j)) for i in range(4) for j in range(3)}
            NU = len(units)

            def load_w(u):
                hp, g = units[u]
                cq = g * 512 + hp * 128
                ub = u % 2
                for j in range(3):
                    dma("pool", Wqkv.ap[:, ub, j], win_v[:, :, j * 1536 + cq:j * 1536 + cq + 128], [], [Wqkv.h(ub)])

            def emit_transposes(u, n):
                ub = u % 2
                rs = n % 2
                s = n % 4
                tpv = bank(3).bitcast(BF16).rearrange("p (b j n) -> p b j n", b=4, j=2)
                if CFG.get("drain", 0):
                    S.op("pe", lambda e: e.drain(), [rotb.h(rs)], [])
                for j in (1, 0):
                    tr(tpv[:, s, j, :], rotb.ap[:, rs, j, :], [rotb.h((rs, j))], [hB[3]])
                if s == 3:
                    j4 = n // 4
                    tsl = slice(j4 * 512, (j4 + 1) * 512)
                    hq = [qkT.h((ub, j4))]
                    for hh_ in range(2):
                        ts("dve", qkT.ap[:, ub, hh_, tsl].rearrange("p (b n) -> p b n", b=4), tpv[:, :, 0, :], hm_t[:, hh_:hh_ + 1], None,
                           ALU.mult, None, [hB[3], h_hm], hq)
                    cp("dve", qkT.ap[:, ub, 2, tsl].rearrange("p (b n) -> p b n", b=4), tpv[:, :, 1, :], [hB[3]], hq)

            def p_step(u, n):
                hp, g = units[u]
                ub = u % 2
                gb = g * 16 + n
                rs = n % 2
                cs = slice(0, 128)
                hsl = {j: hB[j] for j in range(3)}
                for j in (0, 2, 1):
                    for kc in range(8):
                        lt, hl = tok_sel(g, n, kc)
                        mm(bank(j)[:, cs], lt, Wqkv.ap[:, ub, j, kc, :], kc == 0, kc == 7, hl + [Wqkv.h(ub)], [hB[j]])
                    if j == 2:
                        act(Vaug.ap[:, ub, n, :, 0:64], bank(2)[:, cs].rearrange("p (h c) -> p h c", h=2), AF.Copy, [hsl[2]], [Vaug.h(ub)])
                        continue
                    bk = bank(j)[:, cs]
                    pst = bk.ap[0][0]
                    x4 = bass.AP(bk.tensor, bk.offset, [[pst, 128], [32, 4], [1, 32]])
                    x_hi = bass.AP(bk.tensor, bk.offset + 32, [[pst, 128], [64, 2], [1, 32]])
                    x_lo = bass.AP(bk.tensor, bk.offset, [[pst, 128], [64, 2], [1, 32]])
                    r1 = rot1.ap[:, rs, j]
                    r2 = rot2.ap[:, rs, j]
                    r1v = r1.rearrange("p (q f) -> p q f", q=4)
                    r2v = r2.rearrange("p (h q f) -> p h q f", h=2, q=2)
                    cb = Ct.ap[:, gb, :].unsqueeze(1).to_broadcast([128, 4, 32])
                    sb_ = St.ap[:, gb, :].unsqueeze(1).to_broadcast([128, 2, 32])
                    snb = Sn.ap[:, gb, :].unsqueeze(1).to_broadcast([128, 2, 32])
                    tt("dve", r1v, x4, cb, ALU.mult, [hsl[j], Ct.h()], [rot1.h((rs, j))])
                    tt("dve", r2v[:, :, 0, :], x_hi, snb, ALU.mult, [hsl[j], Sn.h()], [rot2.h((rs, j, 0))])
                    tt("dve", r2v[:, :, 1, :], x_lo, sb_, ALU.mult, [hsl[j], St.h()], [rot2.h((rs, j, 1))])
                    tt("pool" if j == 0 else "dve", rotb.ap[:, rs, j], r1, r2, ALU.add, [rot1.h((rs, j)), rot2.h((rs, j, 0)), rot2.h((rs, j, 1))], [rotb.h((rs, j))])
                if CFG.get("tdelay", 0) == 0:
                    emit_transposes(u, n)
                elif n >= 1:
                    emit_transposes(u, n - 1)

            def emit_st(u, k):
                hp, g = units[u]
                ub = u % 2
                hh, n = k // 16, k % 16
                pb = 64 * hh
                nq = 256 if has_next(g, n) else 128
                sbk = 4 + k % 2
                sps = bank(sbk)[:, 0:nq]
                qh = [qkT.h((ub, n // 4))] + ([qkT.h((ub, (n + 1) // 4))] if nq == 256 else [])
                mm(sps, qkT.ap[:, ub, 2, n * 128:(n + 1) * 128], qkT.ap[:, ub, hh, n * 128:n * 128 + nq],
                   True, True, qh, [hB[sbk]])
                act(PTb.ap[:, k % 4, 0:nq], sps, AF.Exp, [hB[sbk]], [PTb.h(k % 4)], scale=0.125)
                tt("pool", PTb.ap[:, k % 4, 0:nq], PTb.ap[:, k % 4, 0:nq], mask_b[:, 0:nq], ALU.mult, [PTb.h(k % 4), h_mask], [PTb.h(k % 4)])

            def emit_pv(u, k):
                hp, g = units[u]
                ub = u % 2
                hh, n = k // 16, k % 16
                nq = 256 if has_next(g, n) else 128
                ps_ = k % 4
                ob = 6 + (n // 4) % 2
                mm(bank(ob)[0:65, (n % 4) * 128:(n % 4 + 1) * 128], Vaug.ap[:, ub, n, hh, 0:65], PTb.ap[:, ps_, 0:128],
                   not has_prev(g, n), True, [Vaug.h(ub), PTb.h(ps_)], [hB[ob]])
                if nq == 256:
                    ob2 = 6 + ((n + 1) // 4) % 2
                    mm(bank(ob2)[0:65, ((n + 1) % 4) * 128:((n + 1) % 4 + 1) * 128], Vaug.ap[:, ub, n, hh, 0:65], PTb.ap[:, ps_, 128:256],
                       True, False, [Vaug.h(ub), PTb.h(ps_)], [hB[ob2]])
                if n % 4 == 3:
                    j4 = n // 4
                    src = bank(ob)[0:65, :]
                    if g == 0:
                        cp("dve", acc.ap[0:65, hh, j4 * 512:(j4 + 1) * 512], src, [hB[ob]], [acc.h(hh)])
                    else:
                        if g == 1:
                            dst = acc.ap[0:65, hh, j4:T:4]
                            srcv = src
                        else:
                            dst = acc.ap[0:65, hh, :].rearrange("p (i r) -> p r i", r=16)[:, 4 * j4:4 * j4 + 4, :]
                            srcv = src.rearrange("p (a i) -> p a i", a=4)
                        tt("dve", dst, srcv, dst, ALU.add, [hB[ob], acc.h(hh)], [acc.h(hh)])

            def normalize(hp):
                for hh in range(2):
                    for tg in range(4):
                        lb = bank(4 + tg % 2)
                        hb_ = hB[4 + tg % 2]
                        mm(lb[0:64, :], sel_f[0:65, :], acc.ap[0:65, hh, tg * 512:(tg + 1) * 512], True, True, [h_sel, acc.h(hh)], [hb_])
                        act(lnL.ap[0:64, :], lb[0:64, :], AF.Ln, [hb_], [lnL.h()])
                        act(rL.ap[0:64, :], lnL.ap[0:64, :], AF.Exp, [lnL.h()], [rL.h()], scale=-1.0)
                        tt("dve", yat.ap[0:64, hp * 2 + hh, tg * 512:(tg + 1) * 512], acc.ap[0:64, hh, tg * 512:(tg + 1) * 512], rL.ap[0:64, :],
                           ALU.mult, [acc.h(hh), rL.h()], [yat.h()])

            ck("memsets")
            load_w(0)
            load_w(1)
            ck("loadw")
            for n in range(NB):
                p_step(0, n)
                ck("ps%d" % n)
            if CFG.get("tdelay", 0):
                emit_transposes(0, NB - 1)
            ck("p0")
            LA = 2
            for u in range(NU):
                if u + 2 < NU:
                    load_w(u + 2)
                for k in range(LA):
                    emit_st(u, k)
                for n in range(NB):
                    if u + 1 < NU:
                        p_step(u + 1, n)
                    for k in (2 * n, 2 * n + 1):
                        if k + LA < 32:
                            emit_st(u, k + LA)
                        emit_pv(u, k)
                if u + 1 < NU and CFG.get("tdelay", 0):
                    emit_transposes(u + 1, NB - 1)
                ck("unit%d" % u)
                if u == NU - 2:
                    WvG_pref = AR.tile("WvG", SCR + 68864, 16384, BF16, "p (k n) -> p k n", k=8)
                    dma("pool", WvG_pref.ap, win_v[:, :, 5632:6656], [], [WvG_pref.h()])
                if units[u][1] == 2:
                    normalize(units[u][0])
                ck("unitn%d" % u)

            ck("attn")
            vgn = AR.tile("vgn", SCR + 0, 32768, BF16, "p (b f) -> p b f", b=NB)
            WvG = WvG_pref
            Wuc = AR.tile("Wuc", SCR + 32768, 16384, BF16, "p (s k n) -> p s k n", s=8, k=8)
            gv = AR.tile("gv", SCR + 49152, 4096, F32)
            uT = AR.tile("uT", SCR + 53248, 4096, F32, "p (s n) -> p s n", s=2)
            tmpm = AR.tile("tmpm", SCR + 57344, 4096, F32, "p (s n) -> p s n", s=2)
            TB = AR.tile("TB", SCR + 61440, 4096, F32, "p (g t) -> p g t", g=8)
            wsb = AR.tile("wsb", SCR + 65536, 2048, BF16, "p (g t) -> p g t", g=8)
            bsbc = AR.tile("bsbc", SCR + 85248, 4096, F32, "p (g t) -> p g t", g=8)
            wsf = AR.tile("wsf", SCR + 89344, 4096, F32, "p (g t) -> p g t", g=8)
            trf = AR.tile("trf", SCR + 93440, 512, F32)
            gsm = AR.tile("gsm", SCR + 93952, 1024, F32, "p (b n) -> p b n", b=NB)

            dma("sp", wsf.ap, wsT, [], [wsf.h()])
            dma("sp", trf.ap, c_tril, [], [trf.h()])
            dma("sp", bsbc.ap.rearrange("p g t -> p (g t)"), gbs.partition_broadcast(128), [], [bsbc.h()])
            for gg in range(8):
                dma("pool", Wuc.ap[:, gg], win_v[:, :, 4608 + gg * 128:4608 + (gg + 1) * 128], [], [Wuc.h(gg)])
            tt("dve", wsb.ap, wsf.ap, trf.ap.unsqueeze(1).to_broadcast([128, 8, 128]), ALU.mult, [wsf.h(), trf.h()], [wsb.h()])
            for g2 in range(2):
                for gq in range(4):
                    gg = g2 * 4 + gq
                    mm(bank(g2)[:, gq * 128:(gq + 1) * 128], ones_b[:], wsb.ap[:, gg, :], True, True, [h_ones, wsb.h()], [hB[g2]])
                for gq in range(4):
                    gg = g2 * 4 + gq
                    stt(TB.ap[:, gg, :], bank(g2)[:, gq * 128:(gq + 1) * 128], glnb_t[:, gg:gg + 1], bsbc.ap[:, gg, :], ALU.mult, ALU.add,
                        [hB[g2], h_gln, bsbc.h()], [TB.h()])
            for blk in range(NB):
                b0 = 4 + 2 * (blk % 2)
                for kc in range(8):
                    for hf in range(2):
                        mm(bank(b0 + hf), A[:, kc, blk * 128:(blk + 1) * 128], WvG.ap[:, kc, hf * 512:(hf + 1) * 512], kc == 0, kc == 7,
                           [At.h(blk), WvG.h()], [hB[b0 + hf]])
                for hf in range(2):
                    act(gv.ap[:, hf * 512:(hf + 1) * 512], bank(b0 + hf), AF.Gelu, [hB[b0 + hf]], [gv.h()])
                hs = gsm.h(blk)
                S.op("dve", lambda e, blk=blk: e.bn_stats(out=gsm.ap[:, blk, 0:6], in_=gv.ap[:, 0:512]), [gv.h()], [hs])
                S.op("dve", lambda e, blk=blk: e.bn_stats(out=gsm.ap[:, blk, 6:12], in_=gv.ap[:, 512:1024]), [gv.h()], [hs])
                S.op("dve", lambda e, blk=blk: e.bn_aggr(out=gsm.ap[:, blk, 12:14], in_=gsm.ap[:, blk, 0:12]), [hs], [hs])
                ts("dve", gsm.ap[:, blk, 13:14], gsm.ap[:, blk, 13:14], LN_EPS, None, ALU.add, None, [hs], [hs])
                act(gsm.ap[:, blk, 14:15], gsm.ap[:, blk, 13:14], AF.Sqrt, [hs], [hs])
                S.op("dve", lambda e, blk=blk: e.reciprocal(out=gsm.ap[:, blk, 15:16], in_=gsm.ap[:, blk, 14:15]), [hs], [hs])
                ts("dve", vgn.ap[:, blk, :], gv.ap, gsm.ap[:, blk, 12:13], gsm.ap[:, blk, 15:16], ALU.subtract, ALU.mult, [gv.h(), hs], [vgn.h(blk)])
            uc = 0
            for gg in range(8):
                for tg in range(4):
                    ub = uc % 2
                    mb = 2 + uc % 2
                    sl = uc % 2
                    uc += 1
                    for kc in range(8):
                        mm(bank(ub), Wuc.ap[:, gg, kc, :], A[:, kc, tg * 512:(tg + 1) * 512], kc == 0, kc == 7, [Wuc.h(gg)] + hA_tg(tg), [hB[ub]])
                    act(uT.ap[:, sl, :], bank(ub), AF.Gelu, [hB[ub]], [uT.h(sl)])
                    for b4 in range(4):
                        blk = tg * 4 + b4
                        mm(bank(mb)[:, b4 * 128:(b4 + 1) * 128], vgn.ap[:, blk, gg * 128:(gg + 1) * 128], wsb.ap[:, gg, :], True, True,
                           [vgn.h(blk), wsb.h()], [hB[mb]])
                    stt(tmpm.ap[:, sl, :].rearrange("p (b t) -> p b t", b=4), bank(mb).rearrange("p (b t) -> p b t", b=4), glng_t[:, gg:gg + 1],
                        TB.ap[:, gg, :].unsqueeze(1).to_broadcast([128, 4, 128]), ALU.mult, ALU.add, [hB[mb], h_gln, TB.h()], [tmpm.h(sl)])
                    tt("pool", ygm.ap[:, gg, tg * 512:(tg + 1) * 512], tmpm.ap[:, sl, :], uT.ap[:, sl, :], ALU.mult, [tmpm.h(sl), uT.h(sl)], [ygm.h((gg, tg))])

            ck("gmlp")
            mrg = AR.tile("mrg", SCR + 0, 32768, BF16, "p (k t) -> p k t", k=8)
            Wsm = AR.tile("Wsm", SCR + 32768, 16384, BF16, "p (s j k n) -> p s j k n", s=2, j=4, k=8)
            sg = AR.tile("sg", SCR + 49152, 8192, F32, "p (s n) -> p s n", s=4)
            Wo = AR.tile("Wo", SCR + 57344, 16384, BF16, "p (k n) -> p k n", k=8)
            stg = AR.tile("stg", SCR + 73728, 16384, F32, "p (s n) -> p s n", s=4)
            wabv = wab.rearrange("(h p) n -> p h n", p=64)
            wgbv = wgb.rearrange("(k p) n -> p k n", p=128)
            woutv = wout.rearrange("(k p) n -> p k n", p=128)
            spv = spill.rearrange("p (b f) -> p b f", b=NB)

            def load_merge_w(f):
                sb_i = f % 2
                h_ = Wsm.h(sb_i)
                dma("pool", Wsm.ap[:, sb_i, 0], win_v[:, :, 6656 + f * 128:6656 + (f + 1) * 128], [], [h_])
                dma("pool", Wsm.ap[:, sb_i, 1], win_v[:, :, 7680 + f * 128:7680 + (f + 1) * 128], [], [h_])
                dma("pool", Wsm.ap[0:64, sb_i, 2], wabv[:, :, f * 128:(f + 1) * 128], [], [h_])
                dma("pool", Wsm.ap[:, sb_i, 3], wgbv[:, :, f * 128:(f + 1) * 128], [], [h_])

            load_merge_w(0)
            dma("pool", Wo.ap, woutv, [], [Wo.h()])
            for b in range(4):
                dma("sp", stg.ap[:, b, :], spv[:, b, :], [hspill], [stg.h(b)])
            mc = 0
            for f in range(8):
                if f + 1 < 8:
                    load_merge_w(f + 1)
                sb_i = f % 2
                hw = Wsm.h(sb_i)
                for tg in range(4):
                    b0 = 4 * (mc % 2)
                    mc += 1
                    tsl = slice(tg * 512, (tg + 1) * 512)
                    for kc in range(8):
                        mm(bank(b0), Wsm.ap[:, sb_i, 0, kc, :], A[:, kc, tsl], kc == 0, kc == 7, [hw] + hA_tg(tg), [hB[b0]])
                    for kc in range(8):
                        mm(bank(b0 + 1), Wsm.ap[:, sb_i, 1, kc, :], A[:, kc, tsl], kc == 0, kc == 7, [hw] + hA_tg(tg), [hB[b0 + 1]])
                    for h8 in range(8):
                        mm(bank(b0 + 2), Wsm.ap[0:64, sb_i, 2, h8, :], yat.ap[0:64, h8, tsl], h8 == 0, h8 == 7, [hw, yat.h()], [hB[b0 + 2]])
                    for kc in range(8):
                        mm(bank(b0 + 3), Wsm.ap[:, sb_i, 3, kc, :], ygm.ap[:, kc, tsl], kc == 0, kc == 7, [hw, ygm.h((kc, tg))], [hB[b0 + 3]])
                    act(sg.ap[:, 0, :], bank(b0), AF.Sigmoid, [hB[b0], h_bg], [sg.h(0)], bias=bg_t[:, f:f + 1], scale=1.0)
                    act(sg.ap[:, 1, :], bank(b0 + 1), AF.Sigmoid, [hB[b0 + 1], h_bg], [sg.h(1)], bias=bg_t[:, 8 + f:9 + f], scale=1.0)
                    tt("dve", sg.ap[:, 2, :], sg.ap[:, 0, :], bank(b0 + 2), ALU.mult, [sg.h(0), hB[b0 + 2]], [sg.h(2)])
                    tt("dve", sg.ap[:, 3, :], sg.ap[:, 1, :], bank(b0 + 3), ALU.mult, [sg.h(1), hB[b0 + 3]], [sg.h(3)])
                    tt("pool", mrg.ap[:, f, tsl], sg.ap[:, 2, :], sg.ap[:, 3, :], ALU.add, [sg.h(2), sg.h(3)], [mrg.h((f, tg))])
            Rt2 = AR.tile("R2", 0, 65536, F32, "p (b f) -> p b f", b=NB)
            Rt.hs = Rt2.hs
            Rt.inherit = Rt2.inherit
            for blk in range(NB):
                pb = 2 * (blk % 2)
                for kc in range(8):
                    for hf in range(2):
                        mm(bank(pb + hf), mrg.ap[:, kc, blk * 128:(blk + 1) * 128], Wo.ap[:, kc, hf * 512:(hf + 1) * 512], kc == 0, kc == 7,
                           [mrg.h((kc, blk // 4)), Wo.h()], [hB[pb + hf]])
                for hf in range(2):
                    stt(R[:, blk, hf * 512:(hf + 1) * 512], bank(pb + hf), 1.0 / ALPHA, stg.ap[:, blk % 4, hf * 512:(hf + 1) * 512], ALU.mult, ALU.add,
                        [hB[pb + hf], stg.h(blk % 4)], [Rt.h(blk)])
                if blk + 4 < NB:
                    dma("sp", stg.ap[:, blk % 4, :], spv[:, blk + 4, :], [hspill], [stg.h(blk % 4)])
            ck("wout")
            layernorm(ln2g, ln2b, "l2")
            ck("ln2")

            CFG["ln_blockwise"] = 1
            ln3_s1, ln3_s2 = layernorm(ln3g, ln3b, "l3", to_A=False, to_out=True)
            CFG["ln_blockwise"] = 0

            def ln3_block(blk):
                ln3_s1(blk)
                if blk >= 1:
                    ln3_s2(blk - 1)

            ffn(w2g, w2u, w2d, "f2", on_last_block=ln3_block)
            ln3_s2(NB - 1)

        except _Stop:
            pass
        S.op("sp", lambda e: e.nop(), hout_blk, [])
        S.finalize()
    return nc


_CACHE = {}


def _consts():
    ident = np.eye(128, dtype=np.float32)
    j = np.arange(128)[:, None]
    i2 = np.arange(256)[None, :]
    valid = np.where(i2 < 128, j <= i2, j >= i2 - 128)
    maskb = np.where(valid, 1.0, 0.0).astype(np.float32)
    s = np.arange(128)[:, None]
    t = np.arange(128)[None, :]
    tril = (s <= t).astype(np.float32)
    inv_freq = (np.float32(10000.0) ** (-np.arange(0, 64, 2, dtype=np.float32) / np.float32(64))).astype(np.float32)
    invf = np.broadcast_to((inv_freq / np.float32(2 * np.pi)).astype(np.float32)[None, :], (128, 32)).copy()
    sel = np.zeros((128, 64), np.float32)
    sel[64, :] = 1.0
    hm = np.zeros((128, 2), np.float32)
    hm[0:64, 0] = 1.0
    hm[64:128, 1] = 1.0
    return ident, maskb, tril, invf, sel, hm


def _perm_tokens():
    i = np.arange(128)
    cols = []
    for g, d in enumerate(DIL):
        for n in range(16):
            if g == 0:
                tok = n * 128 + i
            elif g == 1:
                r, n2 = n // 4, n % 4
                tok = (n2 * 128 + i) * 4 + r
            else:
                tok = i * 16 + n
            cols.append(tok)
    return np.stack(cols, axis=1)


def kernel(x, positions, ffn1_w_gate, ffn1_w_up, ffn1_w_down, ln1_g, ln1_b, w_in, b_gates,
           gmlp_ln_g, gmlp_ln_b, gmlp_w_s, gmlp_b_s, w_attn_branch, w_gmlp_branch, w_out,
           ln2_g, ln2_b, ffn2_w_gate, ffn2_w_up, ffn2_w_down, ln3_g, ln3_b):
    f = lambda a: np.ascontiguousarray(np.asarray(a, dtype=np.float32))
    x = f(x)
    positions = np.asarray(positions).astype(np.int32)
    if "nc" not in _CACHE:
        _CACHE["nc"] = build_program()
    nc = _CACHE["nc"]
    ident, maskb, tril, invf, sel, hm = _consts()
    perm = _perm_tokens()
    shared = {
        "w1g": f(ffn1_w_gate[0]), "w1u": f(ffn1_w_up[0]), "w1d": f(ffn1_w_down[0]),
        "w2g": f(ffn2_w_gate[0]), "w2u": f(ffn2_w_up[0]), "w2d": f(ffn2_w_down[0]),
        "ln1g": f(ln1_g[0]), "ln1b": f(ln1_b[0]), "ln2g": f(ln2_g[0]), "ln2b": f(ln2_b[0]),
        "ln3g": f(ln3_g[0]), "ln3b": f(ln3_b[0]),
        "w_in": f(w_in[0]),
        "bgT": f(np.asarray(b_gates[0]).reshape(16, 128).T),
        "glngT": f(np.asarray(gmlp_ln_g[0]).reshape(8, 128).T),
        "glnbT": f(np.asarray(gmlp_ln_b[0]).reshape(8, 128).T),
        "wsT": f(np.asarray(gmlp_w_s[0]).transpose(2, 0, 1)),
        "gbs": f(np.asarray(gmlp_b_s[0]).reshape(-1)),
        "wab": f(w_attn_branch[0]), "wgb": f(w_gmlp_branch[0]), "wout": f(w_out[0]),
        "c_ident": ident, "c_mask": maskb, "c_tril": tril, "c_invf": invf, "c_sel": sel, "c_hm": hm,
    }
    in_maps = []
    for b in range(8):
        m = dict(shared)
        m["x"] = np.ascontiguousarray(x[b])
        m["xT"] = np.ascontiguousarray(x[b].T)
        m["pos3"] = np.ascontiguousarray(positions[b][perm]).astype(np.int32)
        in_maps.append(m)
    res = run_bass_kernel_spmd(nc, in_maps, core_ids=list(range(8)))
    return np.stack([np.asarray(r["out"]) for r in res.results], axis=0).astype(np.float32)


if __name__ == "__main__":
    import time
    t0 = time.time()
    nc = build_program()
    print("build ok", time.time() - t0)
```

```python
import contextlib
import numpy as np
import concourse.bass as bass
import concourse.mybir as mybir
from concourse.bass_utils import run_bass_kernel_spmd

F32 = mybir.dt.float32
BF16 = mybir.dt.bfloat16
I32 = mybir.dt.int32
AF = mybir.ActivationFunctionType
ALU = mybir.AluOpType

D = 1024
T = 2048
NB = 16
DFF = 2816
ALPHA = 2.0 ** 0.25
LN_EPS = 1e-5
EPS2 = LN_EPS / (ALPHA * ALPHA)
QUARTERS = [(0, 6), (6, 6), (12, 5), (17, 5)]
DIL = (1, 4, 16)
POOL_DMA_INFLIGHT = 3
CFG = {}


class _Stop(Exception):
    pass


def ck(name):
    if CFG.get('stop') == name:
        raise _Stop()


class H:
    __slots__ = ("name", "lw", "rde", "rdd", "dsem")

    def __init__(self, name, inherit=None):
        self.name = name
        self.lw = None
        self.rde = {}
        self.rdd = list(inherit) if inherit else []
        self.dsem = None

    def pending(self):
        r = list(self.rde.values()) + list(self.rdd)
        if self.lw is not None:
            r.append(self.lw)
        return r


class DSem:
    __slots__ = ("sem", "count")

    def __init__(self, sem):
        self.sem = sem
        self.count = 0


class Op:
    __slots__ = ("eng", "fn", "deps", "signal", "ticket", "is_dma", "dsem", "waits")

    def __init__(self, eng, fn, is_dma=False, dsem=None):
        self.eng = eng
        self.fn = fn
        self.deps = []
        self.signal = False
        self.ticket = None
        self.is_dma = is_dma
        self.dsem = dsem
        self.waits = None


ENGS = ("pe", "act", "dve", "pool", "sp")


class Sched:
    def __init__(self, nc, stack):
        self.nc = nc
        self.stack = stack
        self.ops = {e: [] for e in ENGS}
        self.esem = {e: stack.enter_context(nc.semaphore("s_" + e)) for e in ENGS}
        self.n_dsem = 0
        self.pool_dmas = []

    def new_dsem(self):
        s = self.stack.enter_context(self.nc.semaphore("d%d" % self.n_dsem))
        self.n_dsem += 1
        return DSem(s)

    def _record(self, op, reads, writes):
        deps = []
        for r in reads:
            if r.lw is not None:
                deps.append((r.lw, "raw"))
            for x in r.rdd:
                if x.is_dma and False:
                    pass
        for w in writes:
            if w.lw is not None:
                deps.append((w.lw, "waw"))
            for x in w.rde.values():
                deps.append((x, "war"))
            for x in w.rdd:
                deps.append((x, "war"))
        seen = set()
        for d, kind in deps:
            if d is op or id(d) in seen:
                continue
            if (not d.is_dma) and (not op.is_dma) and d.eng == op.eng:
                if op.eng == "pe" or (kind == "war" and not CFG.get("selfwar", 1)):
                    continue
            seen.add(id(d))
            op.deps.append(d)
            d.signal = True
        for r in reads:
            if op.is_dma:
                r.rdd.append(op)
            else:
                r.rde[op.eng] = op
        for w in writes:
            w.lw = op
            w.rde = {}
            w.rdd = []
        self.ops[op.eng].append(op)
        return op

    def op(self, eng, fn, reads=(), writes=()):
        return self._record(Op(eng, fn), list(reads), list(writes))

    def dma(self, eng, fn, reads=(), writes=(), dsem=None):
        if dsem is None:
            w0 = writes[0]
            if w0.dsem is None:
                w0.dsem = self.new_dsem()
            dsem = w0.dsem
        op = Op(eng, fn, is_dma=True, dsem=dsem)
        op.signal = True
        if eng == "pool":
            if CFG.get('cap', 1) and len(self.pool_dmas) >= POOL_DMA_INFLIGHT:
                op.deps.append(self.pool_dmas[-POOL_DMA_INFLIGHT])
            self.pool_dmas.append(op)
        return self._record(op, list(reads), list(writes))

    def finalize(self):
        nc = self.nc
        for e in ENGS:
            cnt = 0
            for op in self.ops[e]:
                if op.is_dma:
                    op.dsem.count += 16
                    op.ticket = (op.dsem.sem, op.dsem.count)
                elif op.signal:
                    cnt += 1
                    op.ticket = (self.esem[e], cnt)
        for e in ENGS:
            waited = {}
            for op in self.ops[e]:
                need = {}
                for d in op.deps:
                    sem, val = d.ticket
                    k = id(sem)
                    if k not in need or need[k][1] < val:
                        need[k] = (sem, val)
                w = []
                for k, (sem, val) in need.items():
                    if waited.get(k, 0) >= val:
                        continue
                    waited[k] = val
                    w.append((sem, val))
                op.waits = w

        def emit(ename, eng):
            for op in self.ops[ename]:
                for sem, val in op.waits:
                    eng.wait_ge(sem, val)
                ins = op.fn(eng)
                if op.is_dma:
                    ins.then_inc(op.dsem.sem, 16)
                elif op.signal:
                    ins.then_inc(self.esem[ename], 1)

        with nc.Block() as block:
            @block.tensor
            def _(eng):
                emit("pe", eng)

            @block.scalar
            def _(eng):
                emit("act", eng)

            @block.vector
            def _(eng):
                emit("dve", eng)

            @block.gpsimd
            def _(eng):
                emit("pool", eng)

            @block.sync
            def _(eng):
                emit("sp", eng)


class Arena:
    def __init__(self, ap_f32):
        self.ap = ap_f32
        self.live = []

    def tile(self, name, off, nbytes, dtype, pattern=None, **kw):
        assert off % 4 == 0 and nbytes % 4 == 0
        inherit = []
        for t in self.live:
            if t.off < off + nbytes and off < t.off + t.nbytes:
                for h in t.hs.values():
                    inherit.extend(h.pending())
        v = self.ap[:, off // 4:(off + nbytes) // 4]
        if dtype != F32:
            v = v.bitcast(dtype)
        if pattern is not None:
            v = v.rearrange(pattern, **kw)
        t = ATile(name, off, nbytes, v, inherit)
        self.live.append(t)
        return t


class ATile:
    def __init__(self, name, off, nbytes, ap, inherit):
        self.name = name
        self.off = off
        self.nbytes = nbytes
        self.ap = ap
        self.inherit = inherit
        self.hs = {}

    def h(self, key=0):
        if key not in self.hs:
            self.hs[key] = H("%s.%s" % (self.name, key), self.inherit)
        return self.hs[key]

    def hall(self, keys):
        return [self.h(k) for k in keys]


def build_program():
    nc = bass.Bass("TRN2", target_bir_lowering=False)

    def din(name, shape, dt=F32):
        return nc.dram_tensor(name, shape, dt, kind="ExternalInput").ap()

    xT = din("xT", [D, T])
    x = din("x", [T, D])
    pos3 = din("pos3", [128, 48], I32)
    w1g = din("w1g", [D, DFF]); w1u = din("w1u", [D, DFF]); w1d = din("w1d", [DFF, D])
    w2g = din("w2g", [D, DFF]); w2u = din("w2u", [D, DFF]); w2d = din("w2d", [DFF, D])
    ln1g = din("ln1g", [D]); ln1b = din("ln1b", [D])
    ln2g = din("ln2g", [D]); ln2b = din("ln2b", [D])
    ln3g = din("ln3g", [D]); ln3b = din("ln3b", [D])
    w_in = din("w_in", [D, 8704])
    bgT = din("bgT", [128, 16])
    glngT = din("glngT", [128, 8]); glnbT = din("glnbT", [128, 8])
    wsT = din("wsT", [128, 8, 128])
    gbs = din("gbs", [8 * 128])
    wab = din("wab", [512, D]); wgb = din("wgb", [D, D]); wout = din("wout", [D, D])
    c_ident = din("c_ident", [128, 128]); c_mask = din("c_mask", [128, 256]); c_tril = din("c_tril", [128, 128])
    c_invf = din("c_invf", [128, 32]); c_sel = din("c_sel", [128, 64]); c_hm = din("c_hm", [128, 2])
    out = nc.dram_tensor("out", [T, D], F32, kind="ExternalOutput").ap()
    spill = nc.dram_tensor("spill", [128, NB * D], F32, kind="Internal").ap()
    hspill = H("spill")

    with contextlib.ExitStack() as st:
        S = Sched(nc, st)
        ARENA_BYTES = 65536 + 32768 + 110592
        arena_t = st.enter_context(nc.sbuf_tensor("arena", [128, ARENA_BYTES // 4], F32))
        AR = Arena(arena_t[:])
        SCR = 65536 + 32768

        def sbt(name, shape, dt):
            return st.enter_context(nc.sbuf_tensor(name, shape, dt))

        ident_b = sbt("ident_b", [128, 128], BF16); h_ident = H("ident")
        mask_b = sbt("mask_b", [128, 256], BF16); h_mask = H("mask")
        ones_b = sbt("ones_b", [128, 128], BF16); h_ones = H("ones")
        sel_f = sbt("sel_f", [128, 64], F32); h_sel = H("sel")
        eps_t = sbt("eps_t", [128, 1], F32); h_eps = H("eps")
        pi_t = sbt("pi_t", [128, 1], F32); h_pi = H("pi")
        hm_t = sbt("hm_t", [128, 2], F32); h_hm = H("hm")
        bg_t = sbt("bg_t", [128, 16], F32); h_bg = H("bg")
        glng_t = sbt("glng_t", [128, 8], F32); glnb_t = sbt("glnb_t", [128, 8], F32); h_gln = H("gln")

        PP = [st.enter_context(nc.psum_tensor("pp%d" % i, [128, 1024], F32)) for i in range(4)]
        hB = [H("bank%d" % i) for i in range(8)]

        def bank(i):
            return PP[i // 2][:, (i % 2) * 512:(i % 2) * 512 + 512]

        def mm(o, lhsT, rhs, start, stop, reads, writes):
            S.op("pe", lambda e: e.matmul(o, lhsT=lhsT, rhs=rhs, start=start, stop=stop), reads, writes)

        def tr(o, in_, reads, writes):
            S.op("pe", lambda e: e.transpose(o, in_, ident_b[:]), list(reads) + [h_ident], writes)

        def act(o, in_, func, reads, writes, bias=None, scale=None):
            kw = {}
            if bias is not None:
                kw["bias"] = bias
            if scale is not None:
                kw["scale"] = scale
            S.op("act", lambda e: e.activation(out=o, in_=in_, func=func, **kw), reads, writes)

        def tt(eng, o, a, b, op, reads, writes):
            S.op(eng, lambda e: e.tensor_tensor(out=o, in0=a, in1=b, op=op), reads, writes)

        def ts(eng, o, a, s1, s2, op0, op1, reads, writes):
            if op1 is None:
                S.op(eng, lambda e: e.tensor_scalar(out=o, in0=a, scalar1=s1, scalar2=None, op0=op0), reads, writes)
            else:
                S.op(eng, lambda e: e.tensor_scalar(out=o, in0=a, scalar1=s1, scalar2=s2, op0=op0, op1=op1), reads, writes)

        def stt(o, a, sc, b, op0, op1, reads, writes):
            S.op("dve", lambda e: e.scalar_tensor_tensor(out=o, in0=a, scalar=sc, in1=b, op0=op0, op1=op1), reads, writes)

        def cp(eng, o, a, reads, writes):
            S.op(eng, lambda e: e.tensor_copy(out=o, in_=a), reads, writes)

        def memset(eng, o, val, writes):
            S.op(eng, lambda e: e.memset(o, val), [], writes)

        def dma(eng, o, a, reads, writes, dsem=None):
            S.dma(eng, lambda e: e.dma_start(out=o, in_=a), reads, writes, dsem=dsem)

        try:
            dma("pool", ident_b[:], c_ident, [], [h_ident])
            dma("pool", mask_b[:], c_mask, [], [h_mask])
            dma("sp", sel_f[:], c_sel, [], [h_sel])
            dma("sp", hm_t[:], c_hm, [], [h_hm])
            dma("sp", bg_t[:], bgT, [], [h_bg])
            dma("sp", glng_t[:], glngT, [], [h_gln])
            dma("sp", glnb_t[:], glnbT, [], [h_gln])
            memset("dve", ones_b[:], 1.0, [h_ones])
            memset("dve", eps_t[:], EPS2, [h_eps])
            memset("dve", pi_t[:], float(np.pi), [h_pi])

            Rt = AR.tile("R", 0, 65536, F32, "p (b f) -> p b f", b=NB)
            At = AR.tile("A", 65536, 32768, BF16, "p (k t) -> p k t", k=8)
            R = Rt.ap
            A = At.ap

            def hA_tg(tg):
                return At.hall(range(4 * tg, 4 * tg + 4))

            def hA_all():
                return At.hall(range(16))

            xTv = xT.rearrange("(k p) t -> p k t", p=128)
            for j in range(8):
                dma("pool", A[:, :, j * 256:(j + 1) * 256], xTv[:, :, j * 256:(j + 1) * 256], [], At.hall([2 * j, 2 * j + 1]))
            xv = x.rearrange("(b p) f -> p b f", p=128)
            for b4 in range(4):
                dma("sp", R[:, b4 * 4:(b4 + 1) * 4, :], xv[:, b4 * 4:(b4 + 1) * 4, :], [], Rt.hall(range(b4 * 4, b4 * 4 + 4)))

            def ffn(wg, wu, wd, tagp, on_last_block=None):
                actT = AR.tile(tagp + "actT", SCR + 0, 24576, BF16, "p (c t) -> p c t", c=6)
                Wg = AR.tile(tagp + "Wg", SCR + 24576, 12288, BF16, "p (k n) -> p k n", k=8)
                Wu = AR.tile(tagp + "Wu", SCR + 36864, 12288, BF16, "p (k n) -> p k n", k=8)
                Wd = AR.tile(tagp + "Wd", SCR + 49152, 12288, BF16, "p (c n) -> p c n", c=6)
                sil = AR.tile(tagp + "sil", SCR + 61440, 4096, F32, "p (s n) -> p s n", s=2)
                wgv = wg.rearrange("(k p) n -> p k n", p=128)
                wuv = wu.rearrange("(k p) n -> p k n", p=128)
                wdv = wd.rearrange("(c p) n -> p c n", p=128)
                cnt = 0
                for (c0, ncq) in QUARTERS:
                    ncol = ncq * 128
                    dma("pool", Wg.ap[:, :, 0:ncol], wgv[:, :, c0 * 128:c0 * 128 + ncol], [], [Wg.h()])
                    dma("pool", Wu.ap[:, :, 0:ncol], wuv[:, :, c0 * 128:c0 * 128 + ncol], [], [Wu.h()])
                    dma("pool", Wd.ap[:, 0:ncq, :], wdv[:, c0:c0 + ncq, :], [], [Wd.h()])
                    for c in range(ncq):
                        for tg in range(4):
                            bg_, bu_ = cnt % 2, 2 + cnt % 2
                            sl = cnt % 2
                            cnt += 1
                            for kc in range(8):
                                mm(bank(bg_), Wg.ap[:, kc, c * 128:(c + 1) * 128], A[:, kc, tg * 512:(tg + 1) * 512],
                                   kc == 0, kc == 7, [Wg.h()] + hA_tg(tg), [hB[bg_]])
                            for kc in range(8):
                                mm(bank(bu_), Wu.ap[:, kc, c * 128:(c + 1) * 128], A[:, kc, tg * 512:(tg + 1) * 512],
                                   kc == 0, kc == 7, [Wu.h()] + hA_tg(tg), [hB[bu_]])
                            act(sil.ap[:, sl, :], bank(bg_), AF.Silu, [hB[bg_]], [sil.h(sl)])
                            tt("dve", actT.ap[:, c, tg * 512:(tg + 1) * 512], sil.ap[:, sl, :], bank(bu_), ALU.mult,
                               [sil.h(sl), hB[bu_]], [actT.h((c, tg))])
                    for blk in range(NB):
                        pb = 4 + 2 * (blk % 2)
                        for hf in range(2):
                            for c in range(ncq):
                                mm(bank(pb + hf), actT.ap[:, c, blk * 128:(blk + 1) * 128], Wd.ap[:, c, hf * 512:(hf + 1) * 512],
                                   c == 0, c == ncq - 1, [actT.h((c, blk // 4)), Wd.h()], [hB[pb + hf]])
                        for hf in range(2):
                            stt(R[:, blk, hf * 512:(hf + 1) * 512], bank(pb + hf), 0.5 / ALPHA, R[:, blk, hf * 512:(hf + 1) * 512],
                                ALU.mult, ALU.add, [hB[pb + hf], Rt.h(blk)], [Rt.h(blk)])
                        if on_last_block is not None and c0 == QUARTERS[-1][0]:
                            on_last_block(blk)

            LNO = SCR + 97280
            hout_dsem = S.new_dsem()
            hout_blk = [H("out%d" % i) for i in range(NB)]

            def layernorm(g_ap, b_ap, tagp, to_A=True, to_out=False):
                Gb = AR.tile(tagp + "Gb", LNO, 4096, F32)
                Bb = AR.tile(tagp + "Bb", LNO + 4096, 4096, F32)
                hb = AR.tile(tagp + "hb", LNO + 8192, 4096, BF16, "p (s n) -> p s n", s=2)
                sm = AR.tile(tagp + "sm", LNO + 12288, 1024, F32, "p (b n) -> p b n", b=NB)
                dma("sp", Gb.ap, g_ap.partition_broadcast(128), [], [Gb.h()])
                dma("sp", Bb.ap, b_ap.partition_broadcast(128), [], [Bb.h()])

                def stage1(blk):
                    hs = sm.h(blk)
                    S.op("dve", lambda e: e.bn_stats(out=sm.ap[:, blk, 0:6], in_=R[:, blk, 0:512]), [Rt.h(blk)], [hs])
                    S.op("dve", lambda e: e.bn_stats(out=sm.ap[:, blk, 6:12], in_=R[:, blk, 512:1024]), [Rt.h(blk)], [hs])
                    S.op("dve", lambda e: e.bn_aggr(out=sm.ap[:, blk, 12:14], in_=sm.ap[:, blk, 0:12]), [hs], [hs])
                    act(sm.ap[:, blk, 14:15], sm.ap[:, blk, 13:14], AF.Sqrt, [hs, h_eps], [hs], bias=eps_t[:], scale=1.0)

                def stage2(blk):
                    hs = sm.h(blk)
                    S.op("dve", lambda e: e.reciprocal(out=sm.ap[:, blk, 15:16], in_=sm.ap[:, blk, 14:15]), [hs], [hs])
                    stt(R[:, blk, :], R[:, blk, :], sm.ap[:, blk, 12:13], Gb.ap, ALU.subtract, ALU.mult, [Rt.h(blk), hs, Gb.h()], [Rt.h(blk)])
                    stt(R[:, blk, :], R[:, blk, :], sm.ap[:, blk, 15:16], Bb.ap, ALU.mult, ALU.add, [Rt.h(blk), hs, Bb.h()], [Rt.h(blk)])
                    if to_out:
                        dma("sp", out[blk * 128:(blk + 1) * 128, :], R[:, blk, :], [Rt.h(blk)], [hout_blk[blk]], dsem=hout_dsem)
                    if to_A:
                        s = blk % 2
                        act(hb.ap[:, s, :], R[:, blk, :], AF.Copy, [Rt.h(blk)], [hb.h(s)])
                        tb = blk % 2
                        tpv = bank(tb).bitcast(BF16).rearrange("p (k n) -> p k n", k=8)
                        for kc in range(8):
                            tr(tpv[:, kc, :], hb.ap[:, s, kc * 128:(kc + 1) * 128], [hb.h(s)], [hB[tb]])

                def stage3(blk):
                    tb = blk % 2
                    tpv = bank(tb).bitcast(BF16).rearrange("p (k n) -> p k n", k=8)
                    act(A[:, :, blk * 128:(blk + 1) * 128], tpv, AF.Copy, [hB[tb]], [At.h(blk)])

                if CFG.get("ln_blockwise") and not to_A:
                    return stage1, stage2
                for i in range(NB + 2):
                    if i < NB:
                        stage1(i)
                    if 0 <= i - 1 < NB:
                        stage2(i - 1)
                    if to_A and 0 <= i - 2 < NB:
                        stage3(i - 2)

            ffn(w1g, w1u, w1d, "f1")
            ck("ffn1")
            layernorm(ln1g, ln1b, "l1")
            ck("ln1")
            for b4 in range(4):
                dma("sp", spill[:, b4 * 4 * D:(b4 + 1) * 4 * D], arena_t[:, b4 * 4 * D:(b4 + 1) * 4 * D],
                    Rt.hall(range(b4 * 4, b4 * 4 + 4)), [hspill])

            yat = AR.tile("yat", 0, 32768, BF16, "p (h t) -> p h t", h=8)
            ygm = AR.tile("ygm", 32768, 32768, BF16, "p (k t) -> p k t", k=8)
            Wqkv = AR.tile("Wqkv", SCR + 0, 12288, BF16, "p (u j k n) -> p u j k n", u=2, j=3, k=8)
            qkT = AR.tile("qkT", SCR + 12288, 24576, BF16, "p (u j t) -> p u j t", u=2, j=3)
            Vaug = AR.tile("Vaug", SCR + 36864, 8448, BF16, "p (u b h c) -> p u b h c", u=2, b=NB, h=2)
            PTb = AR.tile("PTb", SCR + 61696, 2048, BF16, "p (s n) -> p s n", s=4)
            rot1 = AR.tile("rot1", SCR + 63744, 2048, F32, "p (s j n) -> p s j n", s=2, j=2)
            rot2 = AR.tile("rot2", SCR + 65792, 2048, F32, "p (s j n) -> p s j n", s=2, j=2)
            rotb = AR.tile("rotb", SCR + 67840, 1024, BF16, "p (s j n) -> p s j n", s=2, j=2)
            Ct = AR.tile("Ct", SCR + 68864, 6144, F32, "p (b f) -> p b f", b=48)
            St = AR.tile("St", SCR + 75008, 6144, F32, "p (b f) -> p b f", b=48)
            Sn = AR.tile("Sn", SCR + 81152, 6144, F32, "p (b f) -> p b f", b=48)
            rL = AR.tile("rL", SCR + 87296, 2048, F32)
            lnL = AR.tile("lnL", SCR + 89344, 2048, F32)
            posi = AR.tile("posi", SCR + 91392, 192, I32)
            posf = AR.tile("posf", SCR + 91584, 192, F32)
            invf = AR.tile("invf", SCR + 91776, 128, F32)
            angi = AR.tile("angi", SCR + 45312, 6144, I32, "p (b f) -> p b f", b=48)

            dma("sp", posi.ap, pos3, [], [posi.h()])
            dma("sp", invf.ap, c_invf, [], [invf.h()])
            cp("dve", posf.ap, posi.ap, [posi.h()], [posf.h()])
            for tab, off in ((St, 0.0), (Ct, 0.25)):
                tt("dve", tab.ap, posf.ap.unsqueeze(2).to_broadcast([128, 48, 32]), invf.ap.unsqueeze(1).to_broadcast([128, 48, 32]),
                   ALU.mult, [posf.h(), invf.h()], [tab.h()])
                if off != 0.0:
                    ts("dve", tab.ap, tab.ap, off, None, ALU.add, None, [tab.h()], [tab.h()])
                cp("dve", angi.ap, tab.ap, [tab.h()], [angi.h()])
                cp("dve", Sn.ap, angi.ap, [angi.h()], [Sn.h()])
                tt("dve", tab.ap, tab.ap, Sn.ap, ALU.subtract, [tab.h(), Sn.h()], [tab.h()])
                stt(tab.ap, tab.ap, 0.0, tab.ap, ALU.is_lt, ALU.add, [tab.h()], [tab.h()])
                act(tab.ap, tab.ap, AF.Sin, [tab.h(), h_pi], [tab.h()], bias=pi_t[:], scale=float(-2.0 * np.pi))
            ts("dve", Sn.ap, St.ap, -1.0, None, ALU.mult, None, [St.h()], [Sn.h()])
            ck("tables")

            acc = AR.tile("acc", SCR + 45312, 16384, F32, "p (h t) -> p h t", h=2)
            memset("dve", Vaug.ap.rearrange("p u b h c -> p (u b h) c")[:, :, 64:66], 1.0, [Vaug.h(0), Vaug.h(1)])
            win_v = w_in.rearrange("(k p) n -> p k n", p=128)

            def tok_sel(g, n, kc):
                if g == 0:
                    return A[:, kc, n * 128:(n + 1) * 128], [At.h(n)]
                if g == 1:
                    r, n2 = n // 4, n % 4
                    s0 = n2 * 512 + r
                    return A[:, kc, s0:s0 + 509:4], At.hall(range(4 * n2, 4 * n2 + 4))
                return A[:, kc, n:T:16], hA_all()

            def has_next(g, n):
                return (g == 0 and n < 15) or (g == 1 and n % 4 < 3)

            def has_prev(g, n):
                return (g == 0 and n > 0) or (g == 1 and n % 4 > 0)

            units = [(hp, g) for hp in range(4) for g in range(3)]
            slot_hj = {(i, j): H("slot%d_%d" % (i, j)) for i in range(4) for j in range(3)}
            NU = len(units)

            def load_w(u):
                hp, g = units[u]
                cq = g * 512 + hp * 128
                ub = u % 2
                for j in range(3):
                    dma("pool", Wqkv.ap[:, ub, j], win_v[:, :, j * 1536 + cq:j * 1536 + cq + 128], [], [Wqkv.h(ub)])

            def emit_transposes(u, n):
                ub = u % 2
                rs = n % 2
                s = n % 4
                tpv = bank(3).bitcast(BF16).rearrange("p (b j n) -> p b j n", b=4, j=2)
                if CFG.get("drain", 0):
                    S.op("pe", lambda e: e.drain(), [rotb.h(rs)], [])
                for j in (1, 0):
                    tr(tpv[:, s, j, :], rotb.ap[:, rs, j, :], [rotb.h((rs, j))], [hB[3]])
                if s == 3:
                    j4 = n // 4
                    tsl = slice(j4 * 512, (j4 + 1) * 512)
                    hq = [qkT.h((ub, j4))]
                    for hh_ in range(2):
                        ts("dve", qkT.ap[:, ub, hh_, tsl].rearrange("p (b n) -> p b n", b=4), tpv[:, :, 0, :], hm_t[:, hh_:hh_ + 1], None,
                           ALU.mult, None, [hB[3], h_hm], hq)
                    cp("dve", qkT.ap[:, ub, 2, tsl].rearrange("p (b n) -> p b n", b=4), tpv[:, :, 1, :], [hB[3]], hq)

            def p_step(u, n):
                hp, g = units[u]
                ub = u % 2
                gb = g * 16 + n
                rs = n % 2
                cs = slice(0, 128)
                hsl = {j: hB[j] for j in range(3)}
                for j in (0, 2, 1):
                    for kc in range(8):
                        lt, hl = tok_sel(g, n, kc)
                        mm(bank(j)[:, cs], lt, Wqkv.ap[:, ub, j, kc, :], kc == 0, kc == 7, hl + [Wqkv.h(ub)], [hB[j]])
                    if j == 2:
                        act(Vaug.ap[:, ub, n, :, 0:64], bank(2)[:, cs].rearrange("p (h c) -> p h c", h=2), AF.Copy, [hsl[2]], [Vaug.h(ub)])
                        continue
                    bk = bank(j)[:, cs]
                    pst = bk.ap[0][0]
                    x4 = bass.AP(bk.tensor, bk.offset, [[pst, 128], [32, 4], [1, 32]])
                    x_hi = bass.AP(bk.tensor, bk.offset + 32, [[pst, 128], [64, 2], [1, 32]])
                    x_lo = bass.AP(bk.tensor, bk.offset, [[pst, 128], [64, 2], [1, 32]])
                    r1 = rot1.ap[:, rs, j]
                    r2 = rot2.ap[:, rs, j]
                    r1v = r1.rearrange("p (q f) -> p q f", q=4)
                    r2v = r2.rearrange("p (h q f) -> p h q f", h=2, q=2)
                    cb = Ct.ap[:, gb, :].unsqueeze(1).to_broadcast([128, 4, 32])
                    sb_ = St.ap[:, gb, :].unsqueeze(1).to_broadcast([128, 2, 32])
                    snb = Sn.ap[:, gb, :].unsqueeze(1).to_broadcast([128, 2, 32])
                    tt("dve", r1v, x4, cb, ALU.mult, [hsl[j], Ct.h()], [rot1.h((rs, j))])
                    tt("dve", r2v[:, :, 0, :], x_hi, snb, ALU.mult, [hsl[j], Sn.h()], [rot2.h((rs, j, 0))])
                    tt("dve", r2v[:, :, 1, :], x_lo, sb_, ALU.mult, [hsl[j], St.h()], [rot2.h((rs, j, 1))])
                    tt("pool" if j == 0 else "dve", rotb.ap[:, rs, j], r1, r2, ALU.add, [rot1.h((rs, j)), rot2.h((rs, j, 0)), rot2.h((rs, j, 1))], [rotb.h((rs, j))])
                if CFG.get("tdelay", 0) == 0:
                    emit_transposes(u, n)
                elif n >= 1:
                    emit_transposes(u, n - 1)

            def emit_st(u, k):
                hp, g = units[u]
                ub = u % 2
                hh, n = k // 16, k % 16
                pb = 64 * hh
                nq = 256 if has_next(g, n) else 128
                sbk = 4 + k % 2
                sps = bank(sbk)[:, 0:nq]
                qh = [qkT.h((ub, n // 4))] + ([qkT.h((ub, (n + 1) // 4))] if nq == 256 else [])
                mm(sps, qkT.ap[:, ub, 2, n * 128:(n + 1) * 128], qkT.ap[:, ub, hh, n * 128:n * 128 + nq],
                   True, True, qh, [hB[sbk]])
                act(PTb.ap[:, k % 4, 0:nq], sps, AF.Exp, [hB[sbk]], [PTb.h(k % 4)], scale=0.125)
                tt("pool", PTb.ap[:, k % 4, 0:nq], PTb.ap[:, k % 4, 0:nq], mask_b[:, 0:nq], ALU.mult, [PTb.h(k % 4), h_mask], [PTb.h(k % 4)])

            def emit_pv(u, k):
                hp, g = units[u]
                ub = u % 2
                hh, n = k // 16, k % 16
                nq = 256 if has_next(g, n) else 128
                ps_ = k % 4
                ob = 6 + (n // 4) % 2
                mm(bank(ob)[0:65, (n % 4) * 128:(n % 4 + 1) * 128], Vaug.ap[:, ub, n, hh, 0:65], PTb.ap[:, ps_, 0:128],
                   not has_prev(g, n), True, [Vaug.h(ub), PTb.h(ps_)], [hB[ob]])
                if nq == 256:
                    ob2 = 6 + ((n + 1) // 4) % 2
                    mm(bank(ob2)[0:65, ((n + 1) % 4) * 128:((n + 1) % 4 + 1) * 128], Vaug.ap[:, ub, n, hh, 0:65], PTb.ap[:, ps_, 128:256],
                       True, False, [Vaug.h(ub), PTb.h(ps_)], [hB[ob2]])
                if n % 4 == 3:
                    j4 = n // 4
                    src = bank(ob)[0:65, :]
                    if g == 0:
                        cp("dve", acc.ap[0:65, hh, j4 * 512:(j4 + 1) * 512], src, [hB[ob]], [acc.h(hh)])
                    else:
                        if g == 1:
                            dst = acc.ap[0:65, hh, j4:T:4]
                            srcv = src
                        else:
                            dst = acc.ap[0:65, hh, :].rearrange("p (i r) -> p r i", r=16)[:, 4 * j4:4 * j4 + 4, :]
                            srcv = src.rearrange("p (a i) -> p a i", a=4)
                        tt("dve", dst, srcv, dst, ALU.add, [hB[ob], acc.h(hh)], [acc.h(hh)])

            def normalize(hp):
                for hh in range(2):
                    for tg in range(4):
                        lb = bank(4 + tg % 2)
                        hb_ = hB[4 + tg % 2]
                        mm(lb[0:64, :], sel_f[0:65, :], acc.ap[0:65, hh, tg * 512:(tg + 1) * 512], True, True, [h_sel, acc.h(hh)], [hb_])
                        act(lnL.ap[0:64, :], lb[0:64, :], AF.Ln, [hb_], [lnL.h()])
                        act(rL.ap[0:64, :], lnL.ap[0:64, :], AF.Exp, [lnL.h()], [rL.h()], scale=-1.0)
                        tt("dve", yat.ap[0:64, hp * 2 + hh, tg * 512:(tg + 1) * 512], acc.ap[0:64, hh, tg * 512:(tg + 1) * 512], rL.ap[0:64, :],
                           ALU.mult, [acc.h(hh), rL.h()], [yat.h()])

            ck("memsets")
            load_w(0)
            load_w(1)
            ck("loadw")
            for n in range(NB):
                p_step(0, n)
                ck("ps%d" % n)
            if CFG.get("tdelay", 0):
                emit_transposes(0, NB - 1)
            ck("p0")
            LA = 2
            for u in range(NU):
                if u + 2 < NU:
                    load_w(u + 2)
                for k in range(LA):
                    emit_st(u, k)
                for n in range(NB):
                    if u + 1 < NU:
                        p_step(u + 1, n)
                    for k in (2 * n, 2 * n + 1):
                        if k + LA < 32:
                            emit_st(u, k + LA)
                        emit_pv(u, k)
                if u + 1 < NU and CFG.get("tdelay", 0):
                    emit_transposes(u + 1, NB - 1)
                ck("unit%d" % u)
                if u == NU - 2:
                    WvG_pref = AR.tile("WvG", SCR + 68864, 16384, BF16, "p (k n) -> p k n", k=8)
                    dma("pool", WvG_pref.ap, win_v[:, :, 5632:6656], [], [WvG_pref.h()])
                if units[u][1] == 2:
                    normalize(units[u][0])
                ck("unitn%d" % u)

            ck("attn")
            vgn = AR.tile("vgn", SCR + 0, 32768, BF16, "p (b f) -> p b f", b=NB)
            WvG = WvG_pref
            Wuc = AR.tile("Wuc", SCR + 32768, 16384, BF16, "p (s k n) -> p s k n", s=8, k=8)
            gv = AR.tile("gv", SCR + 49152, 4096, F32)
            uT = AR.tile("uT", SCR + 53248, 4096, F32, "p (s n) -> p s n", s=2)
            tmpm = AR.tile("tmpm", SCR + 57344, 4096, F32, "p (s n) -> p s n", s=2)
            TB = AR.tile("TB", SCR + 61440, 4096, F32, "p (g t) -> p g t", g=8)
            wsb = AR.tile("wsb", SCR + 65536, 2048, BF16, "p (g t) -> p g t", g=8)
            bsbc = AR.tile("bsbc", SCR + 85248, 4096, F32, "p (g t) -> p g t", g=8)
            wsf = AR.tile("wsf", SCR + 89344, 4096, F32, "p (g t) -> p g t", g=8)
            trf = AR.tile("trf", SCR + 93440, 512, F32)
            gsm = AR.tile("gsm", SCR + 93952, 1024, F32, "p (b n) -> p b n", b=NB)

            dma("sp", wsf.ap, wsT, [], [wsf.h()])
            dma("sp", trf.ap, c_tril, [], [trf.h()])
            dma("sp", bsbc.ap.rearrange("p g t -> p (g t)"), gbs.partition_broadcast(128), [], [bsbc.h()])
            for gg in range(8):
                dma("pool", Wuc.ap[:, gg], win_v[:, :, 4608 + gg * 128:4608 + (gg + 1) * 128], [], [Wuc.h(gg)])
            tt("dve", wsb.ap, wsf.ap, trf.ap.unsqueeze(1).to_broadcast([128, 8, 128]), ALU.mult, [wsf.h(), trf.h()], [wsb.h()])
            for g2 in range(2):
                for gq in range(4):
                    gg = g2 * 4 + gq
                    mm(bank(g2)[:, gq * 128:(gq + 1) * 128], ones_b[:], wsb.ap[:, gg, :], True, True, [h_ones, wsb.h()], [hB[g2]])
                for gq in range(4):
                    gg = g2 * 4 + gq
                    stt(TB.ap[:, gg, :], bank(g2)[:, gq * 128:(gq + 1) * 128], glnb_t[:, gg:gg + 1], bsbc.ap[:, gg, :], ALU.mult, ALU.add,
                        [hB[g2], h_gln, bsbc.h()], [TB.h()])
            for blk in range(NB):
                b0 = 4 + 2 * (blk % 2)
                for kc in range(8):
                    for hf in range(2):
                        mm(bank(b0 + hf), A[:, kc, blk * 128:(blk + 1) * 128], WvG.ap[:, kc, hf * 512:(hf + 1) * 512], kc == 0, kc == 7,
                           [At.h(blk), WvG.h()], [hB[b0 + hf]])
                for hf in range(2):
                    act(gv.ap[:, hf * 512:(hf + 1) * 512], bank(b0 + hf), AF.Gelu, [hB[b0 + hf]], [gv.h()])
                hs = gsm.h(blk)
                S.op("dve", lambda e, blk=blk: e.bn_stats(out=gsm.ap[:, blk, 0:6], in_=gv.ap[:, 0:512]), [gv.h()], [hs])
                S.op("dve", lambda e, blk=blk: e.bn_stats(out=gsm.ap[:, blk, 6:12], in_=gv.ap[:, 512:1024]), [gv.h()], [hs])
                S.op("dve", lambda e, blk=blk: e.bn_aggr(out=gsm.ap[:, blk, 12:14], in_=gsm.ap[:, blk, 0:12]), [hs], [hs])
                ts("dve", gsm.ap[:, blk, 13:14], gsm.ap[:, blk, 13:14], LN_EPS, None, ALU.add, None, [hs], [hs])
                act(gsm.ap[:, blk, 14:15], gsm.ap[:, blk, 13:14], AF.Sqrt, [hs], [hs])
                S.op("dve", lambda e, blk=blk: e.reciprocal(out=gsm.ap[:, blk, 15:16], in_=gsm.ap[:, blk, 14:15]), [hs], [hs])
                ts("dve", vgn.ap[:, blk, :], gv.ap, gsm.ap[:, blk, 12:13], gsm.ap[:, blk, 15:16], ALU.subtract, ALU.mult, [gv.h(), hs], [vgn.h(blk)])
            uc = 0
            for gg in range(8):
                for tg in range(4):
                    ub = uc % 2
                    mb = 2 + uc % 2
                    sl = uc % 2
                    uc += 1
                    for kc in range(8):
                        mm(bank(ub), Wuc.ap[:, gg, kc, :], A[:, kc, tg * 512:(tg + 1) * 512], kc == 0, kc == 7, [Wuc.h(gg)] + hA_tg(tg), [hB[ub]])
                    act(uT.ap[:, sl, :], bank(ub), AF.Gelu, [hB[ub]], [uT.h(sl)])
                    for b4 in range(4):
                        blk = tg * 4 + b4
                        mm(bank(mb)[:, b4 * 128:(b4 + 1) * 128], vgn.ap[:, blk, gg * 128:(gg + 1) * 128], wsb.ap[:, gg, :], True, True,
                           [vgn.h(blk), wsb.h()], [hB[mb]])
                    stt(tmpm.ap[:, sl, :].rearrange("p (b t) -> p b t", b=4), bank(mb).rearrange("p (b t) -> p b t", b=4), glng_t[:, gg:gg + 1],
                        TB.ap[:, gg, :].unsqueeze(1).to_broadcast([128, 4, 128]), ALU.mult, ALU.add, [hB[mb], h_gln, TB.h()], [tmpm.h(sl)])
                    tt("pool", ygm.ap[:, gg, tg * 512:(tg + 1) * 512], tmpm.ap[:, sl, :], uT.ap[:, sl, :], ALU.mult, [tmpm.h(sl), uT.h(sl)], [ygm.h((gg, tg))])

            ck("gmlp")
            mrg = AR.tile("mrg", SCR + 0, 32768, BF16, "p (k t) -> p k t", k=8)
            Wsm = AR.tile("Wsm", SCR + 32768, 16384, BF16, "p (s j k n) -> p s j k n", s=2, j=4, k=8)
            sg = AR.tile("sg", SCR + 49152, 8192, F32, "p (s n) -> p s n", s=4)
            Wo = AR.tile("Wo", SCR + 57344, 16384, BF16, "p (k n) -> p k n", k=8)
            stg = AR.tile("stg", SCR + 73728, 16384, F32, "p (s n) -> p s n", s=4)
            wabv = wab.rearrange("(h p) n -> p h n", p=64)
            wgbv = wgb.rearrange("(k p) n -> p k n", p=128)
            woutv = wout.rearrange("(k p) n -> p k n", p=128)
            spv = spill.rearrange("p (b f) -> p b f", b=NB)

            def load_merge_w(f):
                sb_i = f % 2
                h_ = Wsm.h(sb_i)
                dma("pool", Wsm.ap[:, sb_i, 0], win_v[:, :, 6656 + f * 128:6656 + (f + 1) * 128], [], [h_])
                dma("pool", Wsm.ap[:, sb_i, 1], win_v[:, :, 7680 + f * 128:7680 + (f + 1) * 128], [], [h_])
                dma("pool", Wsm.ap[0:64, sb_i, 2], wabv[:, :, f * 128:(f + 1) * 128], [], [h_])
                dma("pool", Wsm.ap[:, sb_i, 3], wgbv[:, :, f * 128:(f + 1) * 128], [], [h_])

            load_merge_w(0)
            dma("pool", Wo.ap, woutv, [], [Wo.h()])
            for b in range(4):
                dma("sp", stg.ap[:, b, :], spv[:, b, :], [hspill], [stg.h(b)])
            mc = 0
            for f in range(8):
                if f + 1 < 8:
                    load_merge_w(f + 1)
                sb_i = f % 2
                hw = Wsm.h(sb_i)
                for tg in range(4):
                    b0 = 4 * (mc % 2)
                    mc += 1
                    tsl = slice(tg * 512, (tg + 1) * 512)
                    for kc in range(8):
                        mm(bank(b0), Wsm.ap[:, sb_i, 0, kc, :], A[:, kc, tsl], kc == 0, kc == 7, [hw] + hA_tg(tg), [hB[b0]])
                    for kc in range(8):
                        mm(bank(b0 + 1), Wsm.ap[:, sb_i, 1, kc, :], A[:, kc, tsl], kc == 0, kc == 7, [hw] + hA_tg(tg), [hB[b0 + 1]])
                    for h8 in range(8):
                        mm(bank(b0 + 2), Wsm.ap[0:64, sb_i, 2, h8, :], yat.ap[0:64, h8, tsl], h8 == 0, h8 == 7, [hw, yat.h()], [hB[b0 + 2]])
                    for kc in range(8):
                        mm(bank(b0 + 3), Wsm.ap[:, sb_i, 3, kc, :], ygm.ap[:, kc, tsl], kc == 0, kc == 7, [hw, ygm.h((kc, tg))], [hB[b0 + 3]])
                    act(sg.ap[:, 0, :], bank(b0), AF.Sigmoid, [hB[b0], h_bg], [sg.h(0)], bias=bg_t[:, f:f + 1], scale=1.0)
                    act(sg.ap[:, 1, :], bank(b0 + 1), AF.Sigmoid, [hB[b0 + 1], h_bg], [sg.h(1)], bias=bg_t[:, 8 + f:9 + f], scale=1.0)
                    tt("dve", sg.ap[:, 2, :], sg.ap[:, 0, :], bank(b0 + 2), ALU.mult, [sg.h(0), hB[b0 + 2]], [sg.h(2)])
                    tt("dve", sg.ap[:, 3, :], sg.ap[:, 1, :], bank(b0 + 3), ALU.mult, [sg.h(1), hB[b0 + 3]], [sg.h(3)])
                    tt("pool", mrg.ap[:, f, tsl], sg.ap[:, 2, :], sg.ap[:, 3, :], ALU.add, [sg.h(2), sg.h(3)], [mrg.h((f, tg))])
            Rt2 = AR.tile("R2", 0, 65536, F32, "p (b f) -> p b f", b=NB)
            Rt.hs = Rt2.hs
            Rt.inherit = Rt2.inherit
            for blk in range(NB):
                pb = 2 * (blk % 2)
                for kc in range(8):
                    for hf in range(2):
                        mm(bank(pb + hf), mrg.ap[:, kc, blk * 128:(blk + 1) * 128], Wo.ap[:, kc, hf * 512:(hf + 1) * 512], kc == 0, kc == 7,
                           [mrg.h((kc, blk // 4)), Wo.h()], [hB[pb + hf]])
                for hf in range(2):
                    stt(R[:, blk, hf * 512:(hf + 1) * 512], bank(pb + hf), 1.0 / ALPHA, stg.ap[:, blk % 4, hf * 512:(hf + 1) * 512], ALU.mult, ALU.add,
                        [hB[pb + hf], stg.h(blk % 4)], [Rt.h(blk)])
                if blk + 4 < NB:
                    dma("sp", stg.ap[:, blk % 4, :], spv[:, blk + 4, :], [hspill], [stg.h(blk % 4)])
            ck("wout")
            layernorm(ln2g, ln2b, "l2")
            ck("ln2")

            CFG["ln_blockwise"] = 1
            ln3_s1, ln3_s2 = layernorm(ln3g, ln3b, "l3", to_A=False, to_out=True)
            CFG["ln_blockwise"] = 0

            def ln3_block(blk):
                ln3_s1(blk)
                if blk >= 1:
                    ln3_s2(blk - 1)

            ffn(w2g, w2u, w2d, "f2", on_last_block=ln3_block)
            ln3_s2(NB - 1)

        except _Stop:
            pass
        S.op("sp", lambda e: e.nop(), hout_blk, [])
        S.finalize()
    return nc


_CACHE = {}


def _consts():
    ident = np.eye(128, dtype=np.float32)
    j = np.arange(128)[:, None]
    i2 = np.arange(256)[None, :]
    valid = np.where(i2 < 128, j <= i2, j >= i2 - 128)
    maskb = np.where(valid, 1.0, 0.0).astype(np.float32)
    s = np.arange(128)[:, None]
    t = np.arange(128)[None, :]
    tril = (s <= t).astype(np.float32)
    inv_freq = (np.float32(10000.0) ** (-np.arange(0, 64, 2, dtype=np.float32) / np.float32(64))).astype(np.float32)
    invf = np.broadcast_to((inv_freq / np.float32(2 * np.pi)).astype(np.float32)[None, :], (128, 32)).copy()
    sel = np.zeros((128, 64), np.float32)
    sel[64, :] = 1.0
    hm = np.zeros((128, 2), np.float32)
    hm[0:64, 0] = 1.0
    hm[64:128, 1] = 1.0
    return ident, maskb, tril, invf, sel, hm


def _perm_tokens():
    i = np.arange(128)
    cols = []
    for g, d in enumerate(DIL):
        for n in range(16):
            if g == 0:
                tok = n * 128 + i
            elif g == 1:
                r, n2 = n // 4, n % 4
                tok = (n2 * 128 + i) * 4 + r
            else:
                tok = i * 16 + n
            cols.append(tok)
    return np.stack(cols, axis=1)


def kernel(x, positions, ffn1_w_gate, ffn1_w_up, ffn1_w_down, ln1_g, ln1_b, w_in, b_gates,
           gmlp_ln_g, gmlp_ln_b, gmlp_w_s, gmlp_b_s, w_attn_branch, w_gmlp_branch, w_out,
           ln2_g, ln2_b, ffn2_w_gate, ffn2_w_up, ffn2_w_down, ln3_g, ln3_b):
    f = lambda a: np.ascontiguousarray(np.asarray(a, dtype=np.float32))
    x = f(x)
    positions = np.asarray(positions).astype(np.int32)
    if "nc" not in _CACHE:
        _CACHE["nc"] = build_program()
    nc = _CACHE["nc"]
    ident, maskb, tril, invf, sel, hm = _consts()
    perm = _perm_tokens()
    shared = {
        "w1g": f(ffn1_w_gate[0]), "w1u": f(ffn1_w_up[0]), "w1d": f(ffn1_w_down[0]),
        "w2g": f(ffn2_w_gate[0]), "w2u": f(ffn2_w_up[0]), "w2d": f(ffn2_w_down[0]),
        "ln1g": f(ln1_g[0]), "ln1b": f(ln1_b[0]), "ln2g": f(ln2_g[0]), "ln2b": f(ln2_b[0]),
        "ln3g": f(ln3_g[0]), "ln3b": f(ln3_b[0]),
        "w_in": f(w_in[0]),
        "bgT": f(np.asarray(b_gates[0]).reshape(16, 128).T),
        "glngT": f(np.asarray(gmlp_ln_g[0]).reshape(8, 128).T),
        "glnbT": f(np.asarray(gmlp_ln_b[0]).reshape(8, 128).T),
        "wsT": f(np.asarray(gmlp_w_s[0]).transpose(2, 0, 1)),
        "gbs": f(np.asarray(gmlp_b_s[0]).reshape(-1)),
        "wab": f(w_attn_branch[0]), "wgb": f(w_gmlp_branch[0]), "wout": f(w_out[0]),
        "c_ident": ident, "c_mask": maskb, "c_tril": tril, "c_invf": invf, "c_sel": sel, "c_hm": hm,
    }
    in_maps = []
    for b in range(8):
        m = dict(shared)
        m["x"] = np.ascontiguousarray(x[b])
        m["xT"] = np.ascontiguousarray(x[b].T)
        m["pos3"] = np.ascontiguousarray(positions[b][perm]).astype(np.int32)
        in_maps.append(m)
    res = run_bass_kernel_spmd(nc, in_maps, core_ids=list(range(8)))
    return np.stack([np.asarray(r["out"]) for r in res.results], axis=0).astype(np.float32)


if __name__ == "__main__":
    import time
    t0 = time.time()
    nc = build_program()
    print("build ok", time.time() - t0)
```
